# Optimizing a Trainium2 kernel written in Bass

```python
import math
import jax, jax.numpy as jnp
from jax import lax
import numpy as np

D_MODEL = 1024
BATCH = 1
SEQ = 16384
DEPTH = 1
DEC_BATCH = 32
DEC_SEQ = 2048
PAST_LEN = 128

MIX_WIDTH = D_MODEL
GLA_HEADS = 4
GLA_DV = MIX_WIDTH // 2 // GLA_HEADS
GLA_DK = GLA_DV // 2
GLA_QK = GLA_HEADS * GLA_DK
GLA_V = GLA_HEADS * GLA_DV
GLA_GATE_RANK = 16
GLA_GATE_NORMALIZER = 16.0
GLA_CHUNK = 64
DIFF_HEADS = 4
DIFF_DH = MIX_WIDTH // 2 // DIFF_HEADS // 2
DIFF_QK = DIFF_HEADS * 2 * DIFF_DH
DIFF_V = DIFF_HEADS * 2 * DIFF_DH
ROT_DIM = DIFF_DH // 4
ROPE_THETA = 500000.0
Q_BLOCK = 128
MEM_LEN = 256
XATTN_HEADS = 4
XATTN_DH = D_MODEL // XATTN_HEADS
D_FF = ((8 * D_MODEL + 3 * 256 - 1) // (3 * 256)) * 256
RMS_EPS = 1e-6
IN_SPLIT_SIZES = (GLA_QK, GLA_QK, GLA_V, GLA_GATE_RANK, GLA_GATE_RANK, GLA_V, DIFF_QK, DIFF_QK, DIFF_V)
IN_COLS = GLA_QK * 2 + GLA_V * 2 + GLA_GATE_RANK * 2 + DIFF_QK * 2 + DIFF_V

kernel_name = "hybrid_gla_diffattn_encoder"


def rmsnorm(x, w):
    xf = x.astype(jnp.float32)
    y = xf * lax.rsqrt(jnp.mean(xf * xf, axis=-1, keepdims=True) + RMS_EPS)
    return (y * w.astype(jnp.float32)).astype(x.dtype)


def rope_partial(t, pos):
    inv = ROPE_THETA ** (-jnp.arange(0, ROT_DIM, 2, dtype=jnp.float32) / ROT_DIM)
    ang = pos.astype(jnp.float32)[:, None] * inv[None, :]
    cos = jnp.cos(ang)[:, None, None, :]
    sin = jnp.sin(ang)[:, None, None, :]
    tf = t.astype(jnp.float32)
    half = ROT_DIM // 2
    x1 = tf[..., :half]
    x2 = tf[..., half:ROT_DIM]
    rest = tf[..., ROT_DIM:]
    return jnp.concatenate([x1 * cos - x2 * sin, x2 * cos + x1 * sin, rest], axis=-1).astype(t.dtype)


def gla_chunk_scan(q, k, v, g):
    B, S, H, dk = q.shape
    dv = v.shape[-1]
    C = GLA_CHUNK
    N = S // C

    def to_chunks(t):
        return t.reshape(B, N, C, H, t.shape[-1]).transpose(0, 3, 1, 2, 4)

    q, k, v, g = to_chunks(q), to_chunks(k), to_chunks(v), to_chunks(g)
    b = jnp.cumsum(g, axis=3)
    b_last = b[:, :, :, -1:, :]
    q_e = q * jnp.exp(b)
    k_e = k * jnp.exp(-b)
    k_d = k * jnp.exp(b_last - b)
    mask = jnp.tril(jnp.ones((C, C), dtype=bool))
    A = jnp.where(mask, jnp.einsum('bhnid,bhnjd->bhnij', q_e, k_e), 0.0)
    o_intra = jnp.einsum('bhnij,bhnjv->bhniv', A, v)
    U = jnp.einsum('bhncd,bhncv->bhndv', k_d, v)
    decay = jnp.exp(b_last[:, :, :, 0, :])

    def step(state, inp):
        dec, u = inp
        return dec[..., None] * state + u, state

    _, s_prev = lax.scan(step, jnp.zeros((B, H, dk, dv), jnp.float32),
                         (decay.transpose(2, 0, 1, 3), U.transpose(2, 0, 1, 3, 4)))
    s_prev = s_prev.transpose(1, 2, 0, 3, 4)
    o_inter = jnp.einsum('bhncd,bhndv->bhncv', q_e, s_prev)
    return (o_intra + o_inter).transpose(0, 2, 3, 1, 4).reshape(B, S, H, dv)


def gla_mixer(q, k, v, gf, gb, og, w_gate_up_f, b_gate_f, w_gate_up_b, b_gate_b, gla_norm_w):
    B, S, _ = q.shape
    dt = q.dtype
    f32 = jnp.float32
    q = q.astype(f32).reshape(B, S, GLA_HEADS, GLA_DK) * (GLA_DK ** -0.5)
    k = k.astype(f32).reshape(B, S, GLA_HEADS, GLA_DK)
    v = v.astype(f32).reshape(B, S, GLA_HEADS, GLA_DV)
    g_f = jax.nn.log_sigmoid(gf.astype(f32) @ w_gate_up_f.astype(f32) + b_gate_f.astype(f32)) / GLA_GATE_NORMALIZER
    g_b = jax.nn.log_sigmoid(gb.astype(f32) @ w_gate_up_b.astype(f32) + b_gate_b.astype(f32)) / GLA_GATE_NORMALIZER
    g_f = g_f.reshape(B, S, GLA_HEADS, GLA_DK)
    g_b = g_b.reshape(B, S, GLA_HEADS, GLA_DK)
    o_f = gla_chunk_scan(q, k, v, g_f)
    flip = lambda t: jnp.flip(t, axis=1)
    o_b = flip(gla_chunk_scan(flip(q), flip(k), flip(v), flip(g_b)))
    o = rmsnorm(o_f + o_b, gla_norm_w)
    o = o * jax.nn.silu(og.astype(f32).reshape(B, S, GLA_HEADS, GLA_DV))
    return o.reshape(B, S, GLA_V).astype(dt)


def diff_attention(q, k, v, lam, lam_init, subln_w, pos):
    B, S, _ = q.shape
    q = rope_partial(q.reshape(B, S, DIFF_HEADS, 2, DIFF_DH), pos) * (DIFF_DH ** -0.5)
    k = rope_partial(k.reshape(B, S, DIFF_HEADS, 2, DIFF_DH), pos)
    v = v.reshape(B, S, DIFF_HEADS, 2 * DIFF_DH)
    nb = S // Q_BLOCK
    qb = q.reshape(B, nb, Q_BLOCK, DIFF_HEADS, 2, DIFF_DH).transpose(1, 0, 2, 3, 4, 5)

    def block(qi):
        s = jnp.einsum('bqhcd,bkhcd->bhcqk', qi, k).astype(jnp.float32)
        p = jax.nn.softmax(s, axis=-1)
        p = p[:, :, 0] - lam * p[:, :, 1]
        return jnp.einsum('bhqk,bkhe->bqhe', p.astype(v.dtype), v)

    o = lax.map(block, qb)
    o = o.transpose(1, 0, 2, 3, 4).reshape(B, S, DIFF_HEADS, 2 * DIFF_DH)
    o = rmsnorm(o, subln_w) * (1.0 - lam_init)
    return o.reshape(B, S, DIFF_V)


def memory_cross_attention(h, m, w_xq, w_xkv, w_xo):
    B, S, _ = h.shape
    M = m.shape[1]
    q = (h @ w_xq).reshape(B, S, XATTN_HEADS, XATTN_DH)
    kv = m @ w_xkv
    k = kv[..., :D_MODEL].reshape(B, M, XATTN_HEADS, XATTN_DH)
    v = kv[..., D_MODEL:].reshape(B, M, XATTN_HEADS, XATTN_DH)
    s = jnp.einsum('bshd,bmhd->bhsm', q, k).astype(jnp.float32) * (XATTN_DH ** -0.5)
    p = jax.nn.softmax(s, axis=-1)
    o = jnp.einsum('bhsm,bmhd->bshd', p.astype(v.dtype), v).reshape(B, S, D_MODEL)
    return o @ w_xo


def swiglu(h, w_gate, w_up, w_down):
    return (jax.nn.silu(h @ w_gate) * (h @ w_up)) @ w_down


def encoder_layer(x, mem, p, lam_init):
    B, S, _ = x.shape
    pos = jnp.arange(S)
    h = rmsnorm(x, p['norm_mix_pre'])
    proj = h @ p['w_in']
    idx = []
    acc = 0
    for sz in IN_SPLIT_SIZES[:-1]:
        acc += sz
        idx.append(acc)
    gq, gk, gv, gf, gb, og, dq, dk, dv = jnp.split(proj, idx, axis=-1)
    o_gla = gla_mixer(gq, gk, gv, gf, gb, og, p['w_gate_up_f'], p['b_gate_f'],
                      p['w_gate_up_b'], p['b_gate_b'], p['gla_norm_w'])
    f32 = jnp.float32
    lam = (jnp.exp(jnp.sum(p['lambda_q1'].astype(f32) * p['lambda_k1'].astype(f32)))
           - jnp.exp(jnp.sum(p['lambda_q2'].astype(f32) * p['lambda_k2'].astype(f32))) + lam_init)
    o_diff = diff_attention(dq, dk, dv, lam, lam_init, p['diff_subln_w'], pos)
    mix = jnp.concatenate([o_gla, o_diff], axis=-1) @ p['w_out']
    x = x + rmsnorm(mix, p['norm_mix_post'])
    h = rmsnorm(x, p['norm_xattn_pre'])
    m = rmsnorm(mem, p['norm_mem'])
    x = x + rmsnorm(memory_cross_attention(h, m, p['w_xq'], p['w_xkv'], p['w_xo']), p['norm_xattn_post'])
    h = rmsnorm(x, p['norm_ffn_pre'])
    x = x + rmsnorm(swiglu(h, p['w_ffn_gate'], p['w_ffn_up'], p['w_ffn_down']), p['norm_ffn_post'])
    return x


def setup_inputs(seed: int = 0) -> dict:
    key = jax.random.key(seed)
    keys = jax.random.split(key, 40)
    it = iter(range(40))
    L = DEPTH

    def nrm(shape, scale):
        return scale * jax.random.normal(keys[next(it)], shape, jnp.float32)

    def gain(shape):
        return 1.0 + 0.02 * jax.random.normal(keys[next(it)], shape, jnp.float32)

    return {
        "x_prompt": nrm((BATCH, SEQ, D_MODEL), 1.0),
        "x_sample": nrm((DEC_BATCH, DEC_SEQ, D_MODEL), 1.0),
        "mem_prompt": nrm((BATCH, MEM_LEN, D_MODEL), 1.0),
        "mem_sample": nrm((DEC_BATCH, MEM_LEN, D_MODEL), 1.0),
        "norm_mix_pre": gain((L, D_MODEL)),
        "w_in": nrm((L, D_MODEL, IN_COLS), D_MODEL ** -0.5),
        "w_gate_up_f": nrm((L, GLA_GATE_RANK, GLA_QK), GLA_GATE_RANK ** -0.5),
        "b_gate_f": nrm((L, GLA_QK), 0.1),
        "w_gate_up_b": nrm((L, GLA_GATE_RANK, GLA_QK), GLA_GATE_RANK ** -0.5),
        "b_gate_b": nrm((L, GLA_QK), 0.1),
        "gla_norm_w": gain((L, GLA_DV)),
        "lambda_q1": nrm((L, DIFF_DH), 0.1),
        "lambda_k1": nrm((L, DIFF_DH), 0.1),
        "lambda_q2": nrm((L, DIFF_DH), 0.1),
        "lambda_k2": nrm((L, DIFF_DH), 0.1),
        "diff_subln_w": gain((L, 2 * DIFF_DH)),
        "w_out": nrm((L, MIX_WIDTH, D_MODEL), MIX_WIDTH ** -0.5),
        "norm_mix_post": gain((L, D_MODEL)),
        "norm_xattn_pre": gain((L, D_MODEL)),
        "norm_mem": gain((L, D_MODEL)),
        "w_xq": nrm((L, D_MODEL, D_MODEL), D_MODEL ** -0.5),
        "w_xkv": nrm((L, D_MODEL, 2 * D_MODEL), D_MODEL ** -0.5),
        "w_xo": nrm((L, D_MODEL, D_MODEL), D_MODEL ** -0.5),
        "norm_xattn_post": gain((L, D_MODEL)),
        "norm_ffn_pre": gain((L, D_MODEL)),
        "w_ffn_gate": nrm((L, D_MODEL, D_FF), D_MODEL ** -0.5),
        "w_ffn_up": nrm((L, D_MODEL, D_FF), D_MODEL ** -0.5),
        "w_ffn_down": nrm((L, D_FF, D_MODEL), D_FF ** -0.5),
        "norm_ffn_post": gain((L, D_MODEL)),
    }


def reference(x_prompt, x_sample, mem_prompt, mem_sample, norm_mix_pre, w_in, w_gate_up_f, b_gate_f,
              w_gate_up_b, b_gate_b, gla_norm_w, lambda_q1, lambda_k1, lambda_q2, lambda_k2, diff_subln_w,
              w_out, norm_mix_post, norm_xattn_pre, norm_mem, w_xq, w_xkv, w_xo, norm_xattn_post,
              norm_ffn_pre, w_ffn_gate, w_ffn_up, w_ffn_down, norm_ffn_post):
    y_prompt = x_prompt
    y_sample = x_sample
    for l in range(DEPTH):
        lam_init = 0.8 - 0.6 * math.exp(-0.3 * l)
        p = {
            'norm_mix_pre': norm_mix_pre[l], 'w_in': w_in[l],
            'w_gate_up_f': w_gate_up_f[l], 'b_gate_f': b_gate_f[l],
            'w_gate_up_b': w_gate_up_b[l], 'b_gate_b': b_gate_b[l],
            'gla_norm_w': gla_norm_w[l],
            'lambda_q1': lambda_q1[l], 'lambda_k1': lambda_k1[l],
            'lambda_q2': lambda_q2[l], 'lambda_k2': lambda_k2[l],
            'diff_subln_w': diff_subln_w[l], 'w_out': w_out[l], 'norm_mix_post': norm_mix_post[l],
            'norm_xattn_pre': norm_xattn_pre[l], 'norm_mem': norm_mem[l],
            'w_xq': w_xq[l], 'w_xkv': w_xkv[l], 'w_xo': w_xo[l], 'norm_xattn_post': norm_xattn_post[l],
            'norm_ffn_pre': norm_ffn_pre[l], 'w_ffn_gate': w_ffn_gate[l], 'w_ffn_up': w_ffn_up[l],
            'w_ffn_down': w_ffn_down[l], 'norm_ffn_post': norm_ffn_post[l],
        }
        y_prompt = encoder_layer(y_prompt, mem_prompt, p, lam_init)
        y_sample = encoder_layer(y_sample, mem_sample, p, lam_init)
    return (y_prompt, y_sample)
```

```python
import math
import contextlib
import numpy as np
import concourse.bass as bass
import concourse.mybir as mybir
from concourse.bass_utils import run_bass_kernel_spmd

F32 = mybir.dt.float32
BF16 = mybir.dt.bfloat16
I32 = mybir.dt.int32
AF = mybir.ActivationFunctionType
ALU = mybir.AluOpType
AX = mybir.AxisListType

D = 1024
KC = 8
MEM = 256
DFF = 2816
FC = 22
INC = 3104
EPS = 1e-6
LAM_INIT = 0.2
VW = 130
THETA = 500000.0


class Cfg:
    def __init__(self, UT=2048, NSQ=4, NC=8):
        self.UT = UT
        self.NT = UT // 128
        self.NSQ = NSQ
        self.NC = NC
        self.T = NC * self.NT
        self.QB = min(512, UT)
        self.NQS = self.QB // 128
        self.HB = min(1024, UT)


class Buf:
    __slots__ = ("lw", "rd")

    def __init__(self):
        self.lw = None
        self.rd = {}


def bufs(n):
    return [Buf() for _ in range(n)]


class Sched:
    ENG = ["pe", "dve", "act", "pool", "sp"]

    def __init__(self, nc, nd_sp=28, nd_pool=12):
        self.nc = nc
        self.q = {e: [] for e in self.ENG}
        self.cnt = {e: 0 for e in self.ENG}
        self.waited = {e: {} for e in self.ENG}
        self.dpool = {"sp": ["ds%d" % k for k in range(nd_sp)], "pool": ["dp%d" % k for k in range(nd_pool)]}
        self.dnext = {"sp": 0, "pool": 0}
        for names in self.dpool.values():
            for n in names:
                self.cnt[n] = 0
        self.n_ins = 0

    def _deps(self, eng, reads, writes):
        deps = {}
        for b in reads:
            if b.lw is not None:
                e, i = b.lw
                if deps.get(e, 0) < i:
                    deps[e] = i
        for b in writes:
            if b.lw is not None:
                e, i = b.lw
                if deps.get(e, 0) < i:
                    deps[e] = i
            for e, i in b.rd.items():
                if deps.get(e, 0) < i:
                    deps[e] = i
        waits = []
        w = self.waited[eng]
        for e, i in deps.items():
            if e == eng and eng == "pe":
                continue
            if w.get(e, 0) < i:
                w[e] = i
                waits.append((e, i))
        return waits

    def _mark(self, me, reads, writes):
        e, i = me
        for b in reads:
            if b.rd.get(e, 0) < i:
                b.rd[e] = i
        for b in writes:
            b.lw = me
            b.rd = {}

    def op(self, eng, fn, reads=(), writes=()):
        waits = self._deps(eng, reads, writes)
        self.cnt[eng] += 1
        me = (eng, self.cnt[eng])
        self._mark(me, reads, writes)
        self.q[eng].append((waits, fn, (eng, 1)))

    def dma(self, qeng, out, in_, reads=(), writes=(), **kw):
        k = self.dnext[qeng]
        names = self.dpool[qeng]
        self.dnext[qeng] = (k + 1) % len(names)
        sn = names[k]
        waits = self._deps(qeng, reads, writes)
        prev = self.cnt[sn]
        if prev > 0 and self.waited[qeng].get(sn, 0) < prev:
            self.waited[qeng][sn] = prev
            waits.append((sn, prev))
        self.cnt[sn] += 1
        me = (sn, self.cnt[sn])
        self._mark(me, reads, writes)
        self.q[qeng].append((waits, lambda e: e.dma_start(out=out, in_=in_, **kw), (sn, 16)))

    def barrier(self):
        engs = ("pe", "dve", "act", "pool")
        for e in engs:
            waits = []
            for e2, c in self.cnt.items():
                if e2 == e or c == 0 or e2 == "sp":
                    continue
                if self.waited[e].get(e2, 0) < c:
                    self.waited[e][e2] = c
                    waits.append((e2, c))
            if waits:
                self.q[e].append((waits, None, None))
        waits = []
        for e2, c in self.cnt.items():
            if e2 == "sp" or c == 0:
                continue
            if self.waited["sp"].get(e2, 0) < c:
                self.waited["sp"][e2] = c
                waits.append((e2, c))
        if waits:
            self.q["sp"].append((waits, None, None))

    def final_wait(self):
        for eng in ("sp",):
            waits = [(e2, c) for e2, c in self.cnt.items() if e2 != eng and c > 0]
            self.q[eng].append((waits, None, None))

    def emit(self, block, sems):
        def run(ename, e):
            n = 0
            for waits, fn, inc in self.q[ename]:
                for sn, v in waits:
                    e.wait_ge(sems[sn], v * (16 if sn[0] == "d" and sn != "dve" else 1))
                if fn is not None:
                    ins = fn(e)
                    if inc is not None:
                        ins.then_inc(sems[inc[0]], inc[1])
                    n += 1
            self.n_ins += n

        @block.tensor
        def _(e):
            run("pe", e)

        @block.vector
        def _(e):
            run("dve", e)

        @block.scalar
        def _(e):
            run("act", e)

        @block.gpsimd
        def _(e):
            run("pool", e)

        @block.sync
        def _(e):
            run("sp", e)


class KB:
    def __init__(self, cfg, debug=False):
        self.cfg = cfg
        self.debug = debug
        self.nc = bass.Bass("TRN2", target_bir_lowering=False)
        self.S = Sched(self.nc)
        self.st = contextlib.ExitStack()
        self.dr = {}
        self.arena_off = 0
        self.tog = 0

    def din(self, name, shape, dt=F32):
        t = self.nc.dram_tensor(name, list(shape), dt, kind="ExternalInput").ap()
        self.dr[name] = t
        return t

    def dout(self, name, shape, dt=F32):
        t = self.nc.dram_tensor(name, list(shape), dt, kind="ExternalOutput").ap()
        self.dr[name] = t
        return t

    def dscr(self, name, shape, dt=BF16, dbg=False):
        kind = "ExternalOutput" if (dbg and self.debug) else "Internal"
        t = self.nc.dram_tensor(name, list(shape), dt, kind=kind).ap()
        self.dr[name] = t
        return t

    def sb(self, name, shape, dt):
        return self.st.enter_context(self.nc.sbuf_tensor(name, list(shape), dt))

    def arena_reset(self):
        self.arena_off = 0

    def ar(self, shape, dt):
        n = 1
        for s in shape[1:]:
            n *= s
        nb = n * (2 if dt == F32 else 1)
        nb = (nb + 1) // 2 * 2
        off = self.arena_off
        assert off + nb <= self.ARENA, "arena overflow %d + %d > %d" % (off, nb, self.ARENA)
        self.arena_off = off + nb
        v = self.arena[:, off:off + nb]
        if dt == F32:
            v = v.bitcast(F32)
        v = v[:, 0:n]
        if len(shape) == 3:
            v = v.rearrange("p (a b) -> p a b", a=shape[1])
        elif len(shape) == 4:
            v = v.rearrange("p (a b c) -> p a b c", a=shape[1], b=shape[2])
        return v

    def psum(self):
        while True:
            i = self.ps_next
            self.ps_next = (i + 1) % 8
            if i not in self.ps_reserved:
                break
        return self.ps[i][:], self.ps[i][:].bitcast(BF16), self.psb[i]

    def alt(self):
        self.tog ^= 1
        return "dve" if self.tog else "act"


def copy_op(K, eng, out, in_, reads, writes, scale=None):
    if eng == "act":
        if scale is None:
            K.S.op("act", lambda e: e.activation(out=out, in_=in_, func=AF.Copy), reads, writes)
        else:
            K.S.op("act", lambda e: e.activation(out=out, in_=in_, func=AF.Copy, scale=scale), reads, writes)
    else:
        if scale is None:
            K.S.op(eng, lambda e: e.tensor_copy(out=out, in_=in_), reads, writes)
        else:
            K.S.op(eng, lambda e: e.tensor_scalar(out=out, in0=in_, scalar1=scale, scalar2=None, op0=ALU.mult), reads, writes)


def build(cfg, debug=False):
    K = KB(cfg, debug)
    nc, S, st = K.nc, K.S, K.st
    UT, NT, NSQ, T, QB, NQS, HB = cfg.UT, cfg.NT, cfg.NSQ, cfg.T, cfg.QB, cfg.NQS, cfg.HB
    NU = 1 + NSQ

    xpf = K.din("xpf", [T * 128, D])
    xps = K.din("xps", [UT, D])
    xs = K.din("xs", [NSQ * UT, D])
    memp = K.din("memp", [MEM, D])
    mems = K.din("mems", [NSQ * MEM, D])
    posb_d = K.din("posb", [128, 1])
    mk_d = K.din("mk", [128, 2 * T])
    vecs = {}
    for n, ln in [("n_mix_pre", D), ("n_mix_post", D), ("n_x_pre", D), ("n_mem", D), ("n_x_post", D),
                  ("n_f_pre", D), ("n_f_post", D), ("bg_f", 256), ("bg_b", 256), ("gla_nw", 128),
                  ("lq1", 64), ("lk1", 64), ("lq2", 64), ("lk2", 64), ("sub_w", 128)]:
        vecs[n] = K.din(n, [1, ln])
    w_in = K.din("w_in", [D, INC])
    wgu_f = K.din("wgu_f", [16, 256])
    wgu_b = K.din("wgu_b", [16, 256])
    w_out = K.din("w_out", [D, D])
    w_xq = K.din("w_xq", [D, D])
    w_xkv = K.din("w_xkv", [D, 2 * D])
    w_xo = K.din("w_xo", [D, D])
    w_fg = K.din("w_fg", [D, DFF])
    w_fu = K.din("w_fu", [D, DFF])
    w_fd = K.din("w_fd", [DFF, D])
    yp = K.dout("yp", [UT, D])
    ys = K.dout("ys", [NSQ * UT, D])
    WIN_d = K.dscr("WIN_d", [128, KC, INC])
    WOUT_d = K.dscr("WOUT_d", [128, KC, D])
    WXQ_d = K.dscr("WXQ_d", [128, KC, D])
    WXKV_d = K.dscr("WXKV_d", [128, KC, 2 * D])
    WXO_d = K.dscr("WXO_d", [128, KC, D])
    WGU_d = K.dscr("WGU_d", [FC, 128, 2, KC, 128])
    WD_d = K.dscr("WD_d", [FC, 128, D])
    KT_d = K.dscr("KT_d", [cfg.NC, 4, 128, UT], dbg=True)
    VP_d = K.dscr("VP_d", [cfg.NC, 4, 128, NT * VW], dbg=True)
    MIXT_d = K.dscr("MIXT_d", [NU, 128, KC, UT], dbg=True)
    if debug:
        SDBG_d = K.dout("SDBG_d", [128, 2, 2, 128])

    sems = {n: st.enter_context(nc.semaphore("s_" + n)) for n in list(S.cnt.keys())}
    identb = K.sb("identb", [128, 128], BF16)
    identf = K.sb("identf", [128, 128], F32)
    onesf = K.sb("onesf", [128, 128], F32)
    Tle = K.sb("Tle", [128, 128], F32)
    Tge = K.sb("Tge", [128, 128], F32)
    Tgt = K.sb("Tgt", [128, 128], F32)
    Tlt = K.sb("Tlt", [128, 128], F32)
    Mle = K.sb("Mle", [128, 128], F32)
    Mge = K.sb("Mge", [128, 128], F32)
    zerosb = K.sb("zerosb", [128, 512], BF16)
    wpre = K.sb("wpre", [128, 4, KC], F32)
    wpost = K.sb("wpost", [128, 3, D], F32)
    glaw = K.sb("glaw", [128, 128], F32)
    subw = K.sb("subw", [128, 128], F32)
    wup = K.sb("wup", [128, 512], F32)
    hm = K.sb("hm", [128, 2], F32)
    smallc = K.sb("smallc", [128, 16], F32)
    mkt = K.sb("mkt", [128, 2, T], F32)
    omk = K.sb("omk", [128, 2, T], F32)
    posb = K.sb("posbt", [128, 1], F32)
    hT = K.sb("hT", [128, KC, UT], BF16)
    KMT = K.sb("KMT", [128, KC, MEM], BF16)
    VMP = K.sb("VMP", [128, 2, 4, 258], BF16)
    xst = K.sb("xst", [128, 2, D], F32)
    xnb = K.sb("xnb", [128, 2, D], BF16)
    junk = K.sb("junk", [128, D], BF16)
    sm = K.sb("sm", [128, 64], F32)
    PT = K.sb("PT", [128, 4, QB], BF16)
    tabs = K.sb("tabs", [128, 6, NT, 16], F32)
    K.ARENA = 57400
    K.arena = K.sb("arena", [128, K.ARENA], BF16)
    K.ps = [st.enter_context(nc.psum_tensor("ps%d" % i, [128, 512], F32)) for i in range(8)]
    K.psb = bufs(8)
    K.ps_next = 0
    K.ps_reserved = set()
    block = st.enter_context(nc.Block())

    B_const = Buf()
    b_xst = bufs(2)
    b_xnb = bufs(2)
    b_junk = Buf()
    b_PT = bufs(4)
    b_hT = bufs(NT)
    b_KM = Buf()
    sm_i = [0]
    b_sm = bufs(12)

    def smslot():
        i = sm_i[0]
        sm_i[0] = (i + 1) % 12
        return sm[:, i * 4:(i + 1) * 4], b_sm[i]

    xi = [0]

    def xslot():
        i = xi[0]
        xi[0] = (i + 1) % 2
        return xst[:, i, :], b_xst[i]

    xni = [0]

    def xnslot():
        i = xni[0]
        xni[0] = (i + 1) % 2
        return xnb[:, i, :], b_xnb[i]

    pti = [0]

    def ptslot():
        i = pti[0]
        pti[0] = (i + 1) % 4
        return PT[:, i, :], b_PT[i]

    def setup_consts():
        S.op("pool", lambda e: e.memset(onesf[:], 1.0), (), [B_const])
        S.op("pool", lambda e: e.memset(zerosb[:], 0.0), (), [B_const])
        S.op("pool", lambda e: e.memset(smallc[:], 0.0), (), [B_const])
        S.op("pool", lambda e: e.memset(sm[:], 0.0), (), [B_const])
        S.op("pool", lambda e: e.memset(smallc[:, 1:2], -1.0 / 16), [B_const], [B_const])
        S.op("pool", lambda e: e.memset(smallc[:, 2:6], -0.5), [B_const], [B_const])
        S.op("pool", lambda e: e.memset(wup[:], 0.0), (), [B_const])
        S.op("pool", lambda e: e.memset(hm[:], 0.0), (), [B_const])
        S.op("pool", lambda e: e.memset(hm[0:64, 0:1], 1.0), [B_const], [B_const])
        S.op("pool", lambda e: e.memset(hm[64:128, 1:2], 1.0), [B_const], [B_const])
        S.op("pool", lambda e: e.memset(VMP[:], 1.0), (), [b_KM])

        def asel(out, pat, cm, op, fill=0.0):
            S.op("pool", lambda e: e.affine_select(out=out, in_=onesf[:], pattern=[[pat, 128]], compare_op=op,
                                                    fill=fill, base=0, channel_multiplier=cm), [B_const], [B_const])
        asel(identf[:], 1, -1, ALU.is_equal)
        asel(Mle[:], 1, -1, ALU.is_ge)
        asel(Mge[:], -1, 1, ALU.is_ge)
        asel(Tgt[:], -1, 1, ALU.is_gt)
        asel(Tlt[:], 1, -1, ALU.is_gt)
        S.op("pool", lambda e: e.tensor_copy(out=identb[:], in_=identf[:]), [B_const], [B_const])
        for dst, src in ((Tle, Mle), (Tge, Mge), (Tgt, Tgt), (Tlt, Tlt)):
            S.op("pool", lambda e, dst=dst, src=src: e.tensor_scalar(out=dst[:], in0=src[:], scalar1=-1.0 / 16, scalar2=None,
                                                                     op0=ALU.mult), [B_const], [B_const])
        for k, n in enumerate(["n_mix_pre", "n_x_pre", "n_f_pre", "n_mem"]):
            S.dma("sp", wpre[:, k, :], vecs[n].rearrange("o (c p) -> p (o c)", p=128), (), [B_const],
                  allow_slow_non_contiguous=True)
        for k, n in enumerate(["n_mix_post", "n_x_post", "n_f_post"]):
            S.dma("sp", wpost[:, k, :], vecs[n].to_broadcast([128, D]), (), [B_const])
        S.dma("sp", glaw[:], vecs["gla_nw"].to_broadcast([128, 128]), (), [B_const])
        S.dma("sp", subw[:], vecs["sub_w"].to_broadcast([128, 128]), (), [B_const])
        S.dma("sp", wup[0:16, 0:256], wgu_f[:, :], [B_const], [B_const])
        S.dma("sp", wup[16:32, 256:512], wgu_b[:, :], [B_const], [B_const])
        S.dma("sp", wup[32:33, 0:256], vecs["bg_f"], [B_const], [B_const])
        S.dma("sp", wup[32:33, 256:512], vecs["bg_b"], [B_const], [B_const])
        S.dma("sp", mkt[:].rearrange("p a b -> p (a b)"), mk_d[:, :], (), [B_const])
        S.dma("sp", posb[:], posb_d[:, :], (), [B_const])
        S.op("dve", lambda e: e.tensor_scalar(out=subw[:], in0=subw[:], scalar1=1.0 - LAM_INIT, scalar2=None, op0=ALU.mult),
             [B_const], [B_const])
        S.op("dve", lambda e: e.tensor_scalar(out=omk[:], in0=mkt[:], scalar1=-1.0, scalar2=1.0, op0=ALU.mult, op1=ALU.add),
             [B_const], [B_const])
        lt = sm[0:1, 0:4]
        lv = K.arena[0:1, 0:1024].bitcast(F32)
        for k, n in enumerate(["lq1", "lk1", "lq2", "lk2"]):
            S.dma("sp", lv[:, k * 64:(k + 1) * 64], vecs[n], [B_const], [B_const])
        S.op("dve", lambda e: e.tensor_tensor(out=lv[:, 256:320], in0=lv[:, 0:64], in1=lv[:, 64:128], op=ALU.mult), [B_const], [B_const])
        S.op("dve", lambda e: e.tensor_tensor(out=lv[:, 320:384], in0=lv[:, 128:192], in1=lv[:, 192:256], op=ALU.mult), [B_const], [B_const])
        S.op("dve", lambda e: e.tensor_reduce(out=lt[:, 0:2], in_=lv[:, 256:384].rearrange("p (a b) -> p a b", a=2), axis=AX.X, op=ALU.add),
             [B_const], [B_const])
        S.op("act", lambda e: e.activation(out=lt[:, 0:2], in_=lt[:, 0:2], func=AF.Exp), [B_const], [B_const])
        S.op("dve", lambda e: e.tensor_tensor(out=lt[:, 2:3], in0=lt[:, 1:2], in1=lt[:, 0:1], op=ALU.subtract), [B_const], [B_const])
        S.op("dve", lambda e: e.tensor_scalar(out=lt[:, 2:3], in0=lt[:, 2:3], scalar1=-LAM_INIT, scalar2=None, op0=ALU.add), [B_const], [B_const])
        pf, pbf, pbuf = K.psum()
        S.op("pe", lambda e: e.matmul(pf[:, 0:2], lhsT=onesf[0:1, :], rhs=lt[:, 2:4], start=True, stop=True), [B_const], [pbuf])
        S.op("dve", lambda e: e.tensor_copy(out=smallc[:, 0:1], in_=pf[:, 0:1]), [pbuf, B_const], [B_const])

    inv_freq = [THETA ** (-(2.0 * i) / 16.0) for i in range(8)]

    def make_tables(ntile, tile0_is_posb, cs_out, sc_out, scale, work):
        posi = work[:, 0, :, 0:1].rearrange("p a b -> p (a b)")
        ang = work[:, 1, :, :]
        kk = work[:, 2, :, :]
        tt = work[:, 3, :, :]
        pi_ = K.sb("posi%d" % make_tables.n, [128, ntile], I32)
        pf_ = K.sb("posf%d" % make_tables.n, [128, ntile], F32)
        make_tables.n += 1
        S.op("pool", lambda e: e.iota(pi_[:], pattern=[[128, ntile]], base=0, channel_multiplier=1), (), [B_const])
        S.op("dve", lambda e: e.tensor_copy(out=pf_[:], in_=pi_[:]), [B_const], [B_const])
        if tile0_is_posb:
            S.op("dve", lambda e: e.tensor_scalar(out=pf_[:], in0=pf_[:], scalar1=posb[:, 0:1], scalar2=None, op0=ALU.add), [B_const], [B_const])
        for i in range(8):
            S.op("dve", lambda e, i=i: e.tensor_scalar(out=ang[:, :, i], in0=pf_[:], scalar1=float(np.float32(inv_freq[i])), scalar2=None,
                                                       op0=ALU.mult), [B_const], [B_const])
        MAGIC = 12582912.0
        c1 = float(np.float32(2 * math.pi))
        c2 = float(2 * math.pi - c1)

        def reduce_sin(dst, shift):
            if shift != 0.0:
                S.op("dve", lambda e: e.tensor_scalar(out=tt, in0=ang, scalar1=shift, scalar2=None, op0=ALU.add), [B_const], [B_const])
                src = tt
            else:
                src = ang
            S.op("dve", lambda e: e.tensor_scalar(out=kk, in0=src, scalar1=1.0 / (2 * math.pi), scalar2=MAGIC, op0=ALU.mult, op1=ALU.add), [B_const], [B_const])
            S.op("dve", lambda e: e.tensor_scalar(out=kk, in0=kk, scalar1=-MAGIC, scalar2=None, op0=ALU.add), [B_const], [B_const])
            S.op("dve", lambda e: e.scalar_tensor_tensor(out=tt, in0=kk, scalar=-c1, in1=src, op0=ALU.mult, op1=ALU.add), [B_const], [B_const])
            S.op("dve", lambda e: e.scalar_tensor_tensor(out=tt, in0=kk, scalar=-c2, in1=tt, op0=ALU.mult, op1=ALU.add), [B_const], [B_const])
            S.op("dve", lambda e: e.tensor_scalar(out=kk, in0=tt, scalar1=math.pi, scalar2=-2 * math.pi, op0=ALU.is_gt, op1=ALU.mult), [B_const], [B_const])
            S.op("dve", lambda e: e.tensor_tensor(out=tt, in0=tt, in1=kk, op=ALU.add), [B_const], [B_const])
            S.op("dve", lambda e: e.tensor_scalar(out=kk, in0=tt, scalar1=-math.pi, scalar2=2 * math.pi, op0=ALU.is_lt, op1=ALU.mult), [B_const], [B_const])
            S.op("dve", lambda e: e.tensor_tensor(out=tt, in0=tt, in1=kk, op=ALU.add), [B_const], [B_const])
            S.op("act", lambda e: e.activation(out=tt, in_=tt, func=AF.Sin), [B_const], [B_const])
            for d in dst:
                S.op("dve", lambda e, d=d: e.tensor_scalar(out=d, in0=tt, scalar1=scale, scalar2=None, op0=ALU.mult), [B_const], [B_const])

        reduce_sin([cs_out[:, :, 8:16], sc_out[:, :, 0:8]], 0.0)
        reduce_sin([cs_out[:, :, 0:8], sc_out[:, :, 8:16]], math.pi / 2)
    make_tables.n = 0

    def rstd_from(ss_ap, n_el, out_ap, rb, wb, ncols=1):
        S.op("dve", lambda e: e.tensor_scalar(out=out_ap, in0=ss_ap, scalar1=1.0 / n_el, scalar2=EPS, op0=ALU.mult, op1=ALU.add), rb, wb)
        S.op("pool", lambda e: e.tensor_tensor(out=out_ap, in0=out_ap, in1=smallc[:, 2:2 + ncols], op=ALU.pow), wb + [B_const], wb)

    def load_norm_T(src_ap, dst_hT, dst_buf, keep_x=None):
        if keep_x is None:
            xa, xb_ = xslot()
            S.dma("sp", xa, src_ap, (), [xb_])
        else:
            xa, xb_ = keep_x
        norm_T(xa, xb_, dst_hT, dst_buf)

    def norm_x(xa, xb_, xn=None, xnb_=None):
        st_, sb_ = smslot()
        S.op("act", lambda e: e.activation(out=junk[:], in_=xa, func=AF.Square, accum_out=st_[:, 0:1]), [xb_], [b_junk, sb_])
        rstd_from(st_[:, 0:1], D, st_[:, 1:2], [sb_], [sb_])
        if xn is None:
            xn, xnb_ = xnslot()
        S.op("act", lambda e: e.activation(out=xn, in_=xa, func=AF.Copy, scale=st_[:, 1:2]), [xb_, sb_], [xnb_])
        return xn, xnb_

    def xT_from(xn, xnb_, dst_hT, dst_buf):
        pf, pbf, pbuf = K.psum()

        def tr(e):
            ins = None
            for c in range(KC):
                ins = e.transpose(out=pbf[:, c * 128:(c + 1) * 128], in_=xn[:, c * 128:(c + 1) * 128], identity=identb[:])
            return ins
        S.op("pe", tr, [xnb_, B_const], [pbuf])
        copy_op(K, K.alt(), dst_hT, pbf.rearrange("p (a b) -> p a b", a=KC), [pbuf], [dst_buf])

    def norm_T(xa, xb_, dst_hT, dst_buf):
        xn, xnb_ = norm_x(xa, xb_)
        xT_from(xn, xnb_, dst_hT, dst_buf)

    class XRing:
        def __init__(self, n):
            self.n = n
            self.x = [K.ar([128, D], F32) for _ in range(n)]
            self.xn = [K.ar([128, D], BF16) for _ in range(n)]
            self.bx = bufs(n)
            self.bxn = bufs(n)

        def stage0(self, idx, src_ap):
            j = idx % self.n
            S.dma("sp", self.x[j], src_ap, (), [self.bx[j]])
            norm_x(self.x[j], self.bx[j], self.xn[j], self.bxn[j])

        def get(self, idx):
            j = idx % self.n
            return self.xn[j], self.bxn[j]

    def proj(dst_ps, hT_tile, w_ap, c0, n):
        def f(e):
            ins = None
            for c in range(KC):
                ins = e.matmul(dst_ps[:, 0:n], lhsT=hT_tile[:, c, :], rhs=w_ap[:, c, c0:c0 + n], start=(c == 0), stop=(c == KC - 1))
            return ins
        return f

    def rope(pp, pbuf, dst, dbuf, cs, sc, tmp, tbuf, rest_scale):
        v = pp.rearrange("p (h d) -> p h d", h=8)
        dv_ = dst.rearrange("p (h d) -> p h d", h=8)
        csb = cs.unsqueeze(1).to_broadcast([128, 8, 16])
        scb = sc.unsqueeze(1).to_broadcast([128, 8, 16])
        t = tmp[:, 0:128].rearrange("p (h d) -> p h d", h=8)
        u = tmp[:, 128:256].rearrange("p (h d) -> p h d", h=8)
        S.op("dve", lambda e: e.tensor_tensor(out=t, in0=v[:, :, 0:16], in1=csb, op=ALU.mult), [pbuf, B_const], [tbuf])
        S.op("dve", lambda e: e.tensor_tensor(out=u, in0=v[:, :, 0:16], in1=scb, op=ALU.mult), [pbuf, B_const], [tbuf])
        S.op("dve", lambda e: e.tensor_tensor(out=dv_[:, :, 0:8], in0=t[:, :, 0:8], in1=t[:, :, 8:16], op=ALU.subtract), [tbuf], [dbuf])
        S.op("dve", lambda e: e.tensor_tensor(out=dv_[:, :, 8:16], in0=u[:, :, 0:8], in1=u[:, :, 8:16], op=ALU.add), [tbuf], [dbuf])
        if rest_scale is None:
            S.op("act", lambda e: e.activation(out=dv_[:, :, 16:64], in_=v[:, :, 16:64], func=AF.Copy), [pbuf], [dbuf])
        else:
            S.op("act", lambda e: e.activation(out=dv_[:, :, 16:64], in_=v[:, :, 16:64], func=AF.Copy, scale=rest_scale), [pbuf], [dbuf])

    def transpose4(src, sbuf_, dst_ap, dbuf):
        pf, pbf, pbuf = K.psum()

        def tr(e):
            ins = None
            for c in range(4):
                ins = e.transpose(out=pbf[:, c * 128:(c + 1) * 128], in_=src[:, c * 128:(c + 1) * 128], identity=identb[:])
            return ins
        S.op("pe", tr, [sbuf_, B_const], [pbuf])
        copy_op(K, K.alt(), dst_ap, pbf[:, 0:512].rearrange("p (a b) -> p a b", a=4), [pbuf], [dbuf])

    def glT_proj(hT_tile, hbuf, w_ap, wbuf, c0, glT_ap, glT_buf):
        pf, _, pbuf = K.psum()

        def f(e):
            ins = None
            for c in range(KC):
                ins = e.matmul(pf[:, 0:128], lhsT=w_ap[:, c, c0:c0 + 128], rhs=hT_tile[:, c, :], start=(c == 0), stop=(c == KC - 1))
            return ins
        S.op("pe", f, [hbuf, wbuf], [pbuf])
        S.op("dve", lambda e: e.tensor_copy(out=glT_ap[0:32, :], in_=pf[0:32, 0:128]), [pbuf], [glT_buf])

    def gates_from_glT(glT_ap, glT_buf, A):
        pf2, _, pbuf2 = K.psum()
        S.op("pe", lambda e: e.matmul(pf2[:, 0:512], lhsT=glT_ap[0:33, :], rhs=wup[0:33, :], start=True, stop=True),
             [glT_buf, B_const], [pbuf2])
        S.op("act", lambda e: e.activation(out=A["gp"], in_=pf2[:, 0:512], func=AF.Exp, scale=-1.0), [pbuf2], [A["b_gp"]])
        S.op("act", lambda e: e.activation(out=A["gp"], in_=A["gp"], func=AF.Ln, bias=1.0), [A["b_gp"]], [A["b_gp"]])

    def token_cumsum_kd(A, gk, pb_qk, kd_out, kd_buf, masks=None):
        pf, _, pbuf = K.psum()

        def f(e):
            e.matmul(pf[:, 0:256], lhsT=Tgt[:], rhs=A["gp"][:, 0:256], start=True, stop=True)
            return e.matmul(pf[:, 256:512], lhsT=Tlt[:], rhs=A["gp"][:, 256:512], start=True, stop=True)
        S.op("pe", f, [A["b_gp"], B_const], [pbuf])
        S.op("act", lambda e: e.activation(out=A["ec"], in_=pf[:, 0:512], func=AF.Exp), [pbuf], [A["b_ec"]])
        if masks is None:
            S.op("dve", lambda e: e.tensor_tensor(out=kd_out, in0=A["ec"].rearrange("p (a b) -> p a b", a=2),
                                                  in1=gk.unsqueeze(1).to_broadcast([128, 2, 256]), op=ALU.mult), [A["b_ec"], pb_qk], [kd_buf])
        else:
            for d_ in range(2):
                S.op("dve", lambda e, d_=d_: e.scalar_tensor_tensor(out=kd_out[:, d_, :], in0=A["ec"][:, d_ * 256:(d_ + 1) * 256], scalar=masks[d_],
                                                                    in1=gk, op0=ALU.mult, op1=ALU.mult), [A["b_ec"], pb_qk, B_const], [kd_buf])

    def total_decay(A, dec_out, dec_buf):
        pf, _, pbuf = K.psum()

        def f(e):
            ins = None
            for j in range(4):
                ins = e.matmul(pf[:, j:j + 1], lhsT=A["gp"][:, j * 128:(j + 1) * 128], rhs=smallc[:, 1:2], start=True, stop=True)
            return ins
        S.op("pe", f, [A["b_gp"], B_const], [pbuf])
        S.op("act", lambda e: e.activation(out=dec_out, in_=pf[:, 0:4], func=AF.Exp), [pbuf], [dec_buf])

    def u_matmuls(kd, kd_buf, vg, vg_buf, dirs):
        res = []
        for d_ in dirs:
            pf, _, pbuf = K.psum()

            def f(e, d_=d_, pf=pf):
                ins = None
                for h in range(4):
                    pr = h // 2
                    ins = e.matmul(pf[:, h * 128:(h + 1) * 128], lhsT=kd[:, d_, pr * 128:(pr + 1) * 128], rhs=vg[:, h * 128:(h + 1) * 128],
                                   start=True, stop=True)
                return ins
            S.op("pe", f, [kd_buf, vg_buf], [pbuf])
            res.append((pf, pbuf))
        return res

    b_W = {n: Buf() for n in ["WIN", "WOUT", "WXQ", "WXKV", "WXO", "WGU", "WD"]}

    def weight_prep():
        K.arena_reset()
        stg = [K.ar([128, KC, 512], F32) for _ in range(2)]
        cvt = [K.ar([128, KC, 512], BF16) for _ in range(2)]
        b_stg = bufs(2)
        b_cvt = bufs(2)
        cnt = [0]

        def block(src_w, c0, n, gain_k, stores):
            i = cnt[0] % 2
            cnt[0] += 1
            S.dma("sp", stg[i][:, :, 0:n], src_w.rearrange("(c p) n -> p c n", p=128)[:, :, c0:c0 + n], (), [b_stg[i]])
            if gain_k is None:
                S.op("dve", lambda e: e.tensor_copy(out=cvt[i][:, :, 0:n], in_=stg[i][:, :, 0:n]), [b_stg[i]], [b_cvt[i]])
            else:
                for c in range(KC):
                    S.op("act", lambda e, c=c: e.activation(out=cvt[i][:, c, 0:n], in_=stg[i][:, c, 0:n], func=AF.Copy, scale=wpre[:, gain_k, c:c + 1]),
                         [b_stg[i], B_const], [b_cvt[i]])
            for dst, off, m, wb in stores:
                S.dma("pool", dst, cvt[i][:, :, off:off + m], [b_cvt[i]], [wb])

        for c0 in range(0, INC, 512):
            n = min(512, INC - c0)
            block(w_in, c0, n, 0, [(WIN_d[:, :, c0:c0 + n], 0, n, b_W["WIN"])])
        for c0 in range(0, D, 512):
            block(w_out, c0, 512, None, [(WOUT_d[:, :, c0:c0 + 512], 0, 512, b_W["WOUT"])])
            block(w_xq, c0, 512, 1, [(WXQ_d[:, :, c0:c0 + 512], 0, 512, b_W["WXQ"])])
            block(w_xo, c0, 512, None, [(WXO_d[:, :, c0:c0 + 512], 0, 512, b_W["WXO"])])
        for c0 in range(0, 2 * D, 512):
            block(w_xkv, c0, 512, 3, [(WXKV_d[:, :, c0:c0 + 512], 0, 512, b_W["WXKV"])])
        for gi, wsrc in enumerate((w_fg, w_fu)):
            for c0 in range(0, DFF, 512):
                n = min(512, DFF - c0)
                stores = []
                for j in range(n // 128):
                    fc = c0 // 128 + j
                    stores.append((WGU_d[fc, :, gi, :, :], j * 128, 128, b_W["WGU"]))
                block(wsrc, c0, n, 2, stores)
        for f0 in range(0, FC, 4):
            nf = min(4, FC - f0)
            i = cnt[0] % 2
            cnt[0] += 1
            sv = stg[i].rearrange("p a b -> p (a b)")[:, 0:nf * D].rearrange("p (a b) -> p a b", a=nf)
            cv = cvt[i].rearrange("p a b -> p (a b)")[:, 0:nf * D].rearrange("p (a b) -> p a b", a=nf)
            S.dma("sp", sv, w_fd[f0 * 128:(f0 + nf) * 128, :].rearrange("(a p) n -> p a n", p=128), (), [b_stg[i]])
            S.op("dve", lambda e, cv=cv, sv=sv: e.tensor_copy(out=cv, in_=sv), [b_stg[i]], [b_cvt[i]])
            S.dma("pool", WD_d[f0:f0 + nf].rearrange("a p n -> p a n"), cv, [b_cvt[i]], [b_W["WD"]])

    def gla_work():
        A = {}
        A["gk"] = [K.ar([128, 256], F32) for _ in range(2)]
        A["b_gks"] = bufs(2)
        A["glT"] = [K.ar([128, 128], F32) for _ in range(2)]
        A["b_glTs"] = bufs(2)
        A["gp"] = K.ar([128, 512], F32)
        A["ec"] = K.ar([128, 512], F32)
        for n in ("gp", "ec"):
            A["b_" + n] = Buf()
        for j in range(2):
            S.op("pool", lambda e, j=j: e.memset(A["glT"][j][32:64, :], 1.0), (), [A["b_glTs"][j]])
        return A

    Sst = K.sb("Sst", [128, 2, 2, 128], F32)
    Pst = K.sb("Pst", [128, 2], F32)
    b_Sst = Buf()

    def pre_phase():
        K.arena_reset()
        PC = 512 + 512 + 256 + 512 + 128
        WP = K.ar([128, KC, PC], BF16)
        b_WP = Buf()
        for dst0, src0, n in ((0, 2080, 512), (512, 2592, 512), (1024, 256, 256), (1280, 512, 512), (1792, 1024, 128)):
            S.dma("sp", WP[:, :, dst0:dst0 + n], WIN_d[:, :, src0:src0 + n], [b_W["WIN"]], [b_WP])
        KTs = K.ar([128, 4, UT], BF16)
        VPs = K.ar([128, 4, NT, VW], BF16)
        b_KTs, b_VPs = Buf(), Buf()
        S.op("pool", lambda e: e.memset(VPs[:], 1.0), (), [b_VPs])
        ktab = K.ar([128, 2, T, 16], F32)
        mark_ = K.arena_off
        work = K.ar([128, 4, T, 8], F32)
        make_tables(T, False, ktab[:, 0], ktab[:, 1], 1.0, work)
        S.barrier()
        K.arena_off = mark_
        A = gla_work()
        hTt = [K.ar([128, KC, 128], BF16) for _ in range(2)]
        b_hTt = bufs(2)
        krot = [K.ar([128, 512], BF16) for _ in range(3)]
        b_krot = bufs(3)
        rtmp = K.ar([128, 256], F32)
        b_rtmp = Buf()
        kd = [K.ar([128, 2, 256], BF16) for _ in range(2)]
        b_kd = bufs(2)
        vg = [K.ar([128, 512], BF16) for _ in range(3)]
        b_vg = bufs(3)
        dec = [K.ar([128, 8], F32) for _ in range(2)]
        b_dec = bufs(2)
        S.op("pool", lambda e: e.memset(Sst[:], 0.0), (), [b_Sst])
        S.op("pool", lambda e: e.memset(Pst[:], 1.0), [b_Sst], [b_Sst])

        xr = XRing(2)

        def st1(g):
            seg, lt_ = divmod(g, NT)
            i2, i3 = g % 2, g % 3
            xT_from(*xr.get(g), hTt[i2], b_hTt[i2])
            glT_proj(hTt[i2], b_hTt[i2], WP, b_WP, 1792, A["glT"][i2], A["b_glTs"][i2])
            pg, _, pgb = K.psum()
            S.op("pe", proj(pg, hTt[i2], WP, 1024, 256), [b_hTt[i2], b_WP], [pgb])
            copy_op(K, "act", A["gk"][i2], pg[:, 0:256], [pgb], [A["b_gks"][i2]])
            pgv, _, pgvb = K.psum()
            S.op("pe", proj(pgv, hTt[i2], WP, 1280, 512), [b_hTt[i2], b_WP], [pgvb])
            copy_op(K, "act", vg[i3], pgv[:, 0:512], [pgvb], [b_vg[i3]])
            pk, _, pkb = K.psum()
            S.op("pe", proj(pk, hTt[i2], WP, 0, 512), [b_hTt[i2], b_WP], [pkb])
            rope(pk, pkb, krot[i3], b_krot[i3], ktab[:, 0, g, :], ktab[:, 1, g, :], rtmp, b_rtmp, None)
            pv, _, pvb = K.psum()
            S.op("pe", proj(pv, hTt[i2], WP, 512, 512), [b_hTt[i2], b_WP], [pvb])
            copy_op(K, "dve", VPs[:, :, lt_, 0:128], pv.rearrange("p (h d) -> p h d", h=4), [pvb], [b_VPs])
            if lt_ == NT - 1:
                for h in range(4):
                    S.dma("pool", VP_d[seg, h], VPs[:, h].rearrange("p a b -> p (a b)"), [b_VPs], [b_KV[seg]])

        def st2(g):
            i2 = g % 2
            gates_from_glT(A["glT"][i2], A["b_glTs"][i2], A)
            token_cumsum_kd(A, A["gk"][i2], A["b_gks"][i2], kd[i2], b_kd[i2], masks=(mkt[:, 0, g:g + 1], mkt[:, 1, g:g + 1]))
            dc = dec[i2]
            total_decay(A, dc[:, 0:4], b_dec[i2])
            d4 = dc[:, 0:4].rearrange("p (a b) -> p a b", a=2)
            e4 = dc[:, 4:8].rearrange("p (a b) -> p a b", a=2)
            S.op("dve", lambda e, g=g, d4=d4, e4=e4: e.tensor_tensor(out=e4, in0=d4, in1=mkt[:, :, g:g + 1].to_broadcast([128, 2, 2]), op=ALU.mult),
                 [b_dec[i2], B_const], [b_dec[i2]])
            S.op("dve", lambda e, g=g, e4=e4: e.tensor_tensor(out=e4, in0=e4, in1=omk[:, :, g:g + 1].to_broadcast([128, 2, 2]), op=ALU.add),
                 [b_dec[i2], B_const], [b_dec[i2]])

        def st3(g):
            seg, lt_ = divmod(g, NT)
            i2, i3 = g % 2, g % 3
            dc = dec[i2]
            transpose4(krot[i3], b_krot[i3], KTs[:, :, lt_ * 128:(lt_ + 1) * 128], b_KTs)
            (puf, pufb), (pub, pubb) = u_matmuls(kd[i2], b_kd[i2], vg[i3], b_vg[i3], (0, 1))
            for h in range(4):
                pr, lo = h // 2, (h % 2) * 64
                S.op("dve", lambda e, h=h, pr=pr, lo=lo, puf=puf, dc=dc: e.scalar_tensor_tensor(
                    out=Sst[lo:lo + 64, 0, pr, :], in0=Sst[lo:lo + 64, 0, pr, :], scalar=dc[lo:lo + 64, 4 + pr:5 + pr],
                    in1=puf[lo:lo + 64, h * 128:(h + 1) * 128], op0=ALU.mult, op1=ALU.add), [pufb, b_dec[i2], b_Sst], [b_Sst])
                S.op("dve", lambda e, h=h, pr=pr, lo=lo, pub=pub: e.scalar_tensor_tensor(
                    out=Sst[lo:lo + 64, 1, pr, :], in0=pub[lo:lo + 64, h * 128:(h + 1) * 128], scalar=Pst[lo:lo + 64, pr:pr + 1],
                    in1=Sst[lo:lo + 64, 1, pr, :], op0=ALU.mult, op1=ALU.add), [pubb, b_Sst], [b_Sst])
            S.op("dve", lambda e, dc=dc: e.tensor_tensor(out=Pst[:], in0=Pst[:], in1=dc[:, 6:8], op=ALU.mult), [b_Sst, b_dec[i2]], [b_Sst])
            if lt_ == NT - 1:
                for h in range(4):
                    S.dma("pool", KT_d[seg, h], KTs[:, h, :], [b_KTs], [b_KV[seg]])
        for g in range(-3, T):
            if 0 <= g + 3 < T:
                xr.stage0(g + 3, xpf[(g + 3) * 128:(g + 4) * 128, :])
            if 0 <= g + 2 < T:
                st1(g + 2)
            if 0 <= g + 1 < T:
                st2(g + 1)
            if g >= 0:
                st3(g)
        if debug:
            S.dma("pool", SDBG_d.rearrange("p a b c -> p (a b c)"), Sst[:].rearrange("p a b c -> p (a b c)"), [b_Sst], [Buf()])

    b_KV = bufs(cfg.NC)
    b_MIX = [[Buf() for _ in range(2 * (UT // HB))] for _ in range(NU)]

    def phase_G(u, xsrc):
        K.arena_reset()
        QE = K.ar([128, 2, 2, UT], BF16)
        KE = K.ar([128, 2, 2, UT], BF16)
        VG = K.ar([128, NT, 512], BF16)
        KDB = K.ar([128, NT, 256], BF16)
        SFP = K.ar([128, NT, 2, 128], BF16)
        DECB = K.ar([128, NT, 2], F32)
        b_QE, b_KE, b_VG, b_KDB, b_SFP, b_DECB = bufs(NT), bufs(NT), bufs(NT), bufs(NT), bufs(NT), bufs(NT)
        SF = K.ar([128, 2, 128], F32)
        SB_ = K.ar([128, 2, 128], F32)
        SBb = K.ar([128, 2, 128], BF16)
        b_SF, b_SB, b_SBb = Buf(), Buf(), Buf()
        mark = K.arena_off
        GC = 1056
        WG_ = K.ar([128, KC, GC], BF16)
        b_WG = Buf()
        for c0 in range(0, GC, 512):
            n = min(512, GC - c0)
            S.dma("sp", WG_[:, :, c0:c0 + n], WIN_d[:, :, c0:c0 + n], [b_W["WIN"]], [b_WG])
        A = gla_work()
        glow_sb = [K.ar([128, 32], F32) for _ in range(2)]
        b_glow_sb = bufs(2)
        qk = K.ar([128, 512], BF16)
        b_qk = Buf()
        eqk = K.ar([128, 2, 4, 128], F32)
        b_eqk = Buf()
        kdf = K.ar([128, 2, 256], BF16)
        b_kdf = Buf()
        dec = K.ar([128, 4], F32)
        b_dec = Buf()
        lnb = K.ar([128, 2], F32)
        S.op("pool", lambda e: e.memset(lnb, math.log(0.125)), (), [b_eqk])
        if u == 0:
            S.op("dve", lambda e: e.tensor_copy(out=SF, in_=Sst[:, 0]), [b_Sst], [b_SF])
            S.op("dve", lambda e: e.tensor_copy(out=SB_, in_=Sst[:, 1]), [b_Sst], [b_SB])
        else:
            S.op("pool", lambda e: e.memset(SF, 0.0), (), [b_SF])
            S.op("pool", lambda e: e.memset(SB_, 0.0), (), [b_SB])
        qks = [qk, K.ar([128, 512], BF16)]
        b_qks = [b_qk, Buf()]

        xr = XRing(2)

        def gfront(i):
            tok = slice(i * 128, (i + 1) * 128)
            i2 = i % 2
            xT_from(*xr.get(i), hT[:, :, tok], b_hT[i])
            hTi = hT[:, :, tok]
            pq, _, pqb = K.psum()
            S.op("pe", proj(pq, hTi, WG_, 0, 512), [b_hT[i], b_WG], [pqb])
            pgv, _, pgvb = K.psum()
            S.op("pe", proj(pgv, hTi, WG_, 512, 512), [b_hT[i], b_WG], [pgvb])
            pgl, _, pglb = K.psum()
            S.op("pe", proj(pgl, hTi, WG_, 1024, 32), [b_hT[i], b_WG], [pglb])
            copy_op(K, "act", VG[:, i, :], pgv[:, 0:512], [pgvb], [b_VG[i]])
            copy_op(K, "dve", qks[i2], pq[:, 0:512], [pqb], [b_qks[i2]])
            copy_op(K, "dve", A["gk"][i2], pq[:, 256:512], [pqb], [A["b_gks"][i2]])
            copy_op(K, "dve", glow_sb[i2], pgl[:, 0:32], [pglb], [b_glow_sb[i2]])

        def gback(i):
            tok = slice(i * 128, (i + 1) * 128)
            i2 = i % 2
            qk_ = qks[i2]
            pfg, _, pbg = K.psum()
            S.op("pe", lambda e, pfg=pfg, i2=i2: e.transpose(out=pfg[0:32, 0:128], in_=glow_sb[i2], identity=identf[:]), [b_glow_sb[i2], B_const], [pbg])
            S.op("dve", lambda e, pfg=pfg: e.tensor_copy(out=A["glT"][0][0:32, :], in_=pfg[0:32, 0:128]), [pbg], [A["b_glTs"][0]])
            gates_from_glT(A["glT"][0], A["b_glTs"][0], A)
            token_cumsum_kd(A, A["gk"][i2], A["b_gks"][i2], kdf, b_kdf)
            S.op("pool", lambda e, i=i: e.tensor_copy(out=KDB[:, i, :], in_=kdf[:, 1, :]), [b_kdf], [b_KDB[i]])
            total_decay(A, dec, b_dec)
            S.op("dve", lambda e, i=i: e.tensor_copy(out=DECB[:, i, :], in_=dec[:, 2:4]), [b_dec], [b_DECB[i]])
            pc, _, pcb = K.psum()

            def fcs(e, pc=pc):
                ins = None
                for j in range(4):
                    ins = e.matmul(pc[:, j * 128:(j + 1) * 128], lhsT=A["gp"][:, j * 128:(j + 1) * 128], rhs=(Tle if j < 2 else Tge)[:],
                                   start=True, stop=True)
                return ins
            S.op("pe", fcs, [A["b_gp"], B_const], [pcb])
            S.op("act", lambda e, pc=pc: e.activation(out=eqk[:, 0].rearrange("p a b -> p (a b)"), in_=pc[:, 0:512], func=AF.Exp, bias=lnb[:, 0:1]), [pcb, b_eqk], [b_eqk])
            S.op("act", lambda e, pc=pc: e.activation(out=eqk[:, 1].rearrange("p a b -> p (a b)"), in_=pc[:, 0:512], func=AF.Exp, scale=-1.0), [pcb, b_eqk], [b_eqk])
            pt, ptb, ptbuf = K.psum()

            def trq(e, ptb=ptb, qk_=qk_):
                ins = None
                for c in range(4):
                    ins = e.transpose(out=ptb[:, c * 128:(c + 1) * 128], in_=qk_[:, c * 128:(c + 1) * 128], identity=identb[:])
                return ins
            S.op("pe", trq, [b_qks[i2], B_const], [ptbuf])
            qT = ptb[:, 0:256].rearrange("p (a b) -> p a b", a=2)
            kT = ptb[:, 256:512].rearrange("p (a b) -> p a b", a=2)
            e_q = eqk[:, 0].rearrange("p (d a) b -> p d a b", d=2)
            e_k = eqk[:, 1].rearrange("p (d a) b -> p d a b", d=2)
            for d_ in range(2):
                S.op("dve", lambda e, d_=d_, qT=qT, e_q=e_q, tok=tok: e.tensor_tensor(out=QE[:, d_, :, tok], in0=qT, in1=e_q[:, d_], op=ALU.mult),
                     [ptbuf, b_eqk], [b_QE[i]])
                S.op("dve", lambda e, d_=d_, kT=kT, e_k=e_k, tok=tok: e.tensor_tensor(out=KE[:, d_, :, tok], in0=kT, in1=e_k[:, d_], op=ALU.mult),
                     [ptbuf, b_eqk], [b_KE[i]])
            S.op("dve", lambda e, i=i: e.tensor_copy(out=SFP[:, i], in_=SF), [b_SF], [b_SFP[i]])
            ((puf, pufb),) = u_matmuls(kdf, b_kdf, VG[:, i, :], b_VG[i], (0,))
            for h in range(4):
                pr, lo = h // 2, (h % 2) * 64
                S.op("dve", lambda e, h=h, pr=pr, lo=lo, puf=puf: e.scalar_tensor_tensor(
                    out=SF[lo:lo + 64, pr, :], in0=SF[lo:lo + 64, pr, :], scalar=dec[lo:lo + 64, pr:pr + 1],
                    in1=puf[lo:lo + 64, h * 128:(h + 1) * 128], op0=ALU.mult, op1=ALU.add), [pufb, b_dec, b_SF], [b_SF])
        for i in range(-2, NT):
            if 0 <= i + 2 < NT:
                xr.stage0(i + 2, xsrc[(i + 2) * 128:(i + 3) * 128, :])
            if 0 <= i + 1 < NT:
                gfront(i + 1)
            if i >= 0:
                gback(i)
        if STOP == "G%da" % u:
            return
        S.barrier()
        K.arena_off = mark
        Wog = K.ar([128, KC, 512], BF16)
        b_Wog = Buf()
        S.dma("sp", Wog, WIN_d[:, :, 1056:1568], [b_W["WIN"]], [b_Wog])
        AM = [K.ar([128, 2, 4, 128], BF16) for _ in range(2)]
        b_AM = bufs(2)
        o32 = K.ar([128, 512], F32)
        b_o32 = Buf()
        sq = K.ar([128, 512], F32)
        b_sq = Buf()
        sg = K.ar([128, 512], F32)
        b_sg = Buf()
        ogs = [K.ar([128, 512], BF16) for _ in range(2)]
        b_ogs = bufs(2)
        ogl = [K.ar([128, 512], BF16) for _ in range(2)]
        b_ogl = bufs(2)
        stg = [K.ar([128, 4, QB], BF16) for _ in range(2)]
        b_stg = bufs(2)
        kdb2 = [K.ar([128, 2, 256], BF16) for _ in range(2)]
        b_kdb2 = bufs(2)
        QEm = [K.ar([128, 2, 4, 128], BF16) for _ in range(2)]
        b_QEm = bufs(2)

        def p2front(i):
            tok = slice(i * 128, (i + 1) * 128)
            i2 = i % 2
            pog, _, pogb = K.psum()
            S.op("pe", proj(pog, hT[:, :, tok], Wog, 0, 512), [b_hT[i], b_Wog], [pogb])
            S.op("act", lambda e, pog=pog: e.activation(out=sg, in_=pog[:, 0:512], func=AF.Exp, scale=-1.0), [pogb], [b_sg])
            S.op("act", lambda e: e.activation(out=sg, in_=sg, func=AF.Ln, bias=1.0), [b_sg], [b_sg])
            S.op("act", lambda e: e.activation(out=sg, in_=sg, func=AF.Exp, scale=-1.0), [b_sg], [b_sg])
            S.op("dve", lambda e, pog=pog, i2=i2: e.tensor_tensor(out=ogs[i2], in0=pog[:, 0:512], in1=sg, op=ALU.mult), [pogb, b_sg], [b_ogs[i2]])
            for d_ in range(2):
                S.op("pool", lambda e, d_=d_, i2=i2, tok=tok: e.tensor_tensor(
                    out=QEm[i2][:, d_].rearrange("p (a b) t -> p a b t", a=2), in0=QE[:, d_, :, tok].unsqueeze(2).to_broadcast([128, 2, 2, 128]),
                    in1=hm[:].unsqueeze(1).unsqueeze(3).to_broadcast([128, 2, 2, 128]), op=ALU.mult), [b_QE[i], B_const], [b_QEm[i2]])
            pa = [K.psum() for _ in range(2)]
            for d_ in range(2):
                def fa(e, d_=d_, pf=pa[d_][0], tok=tok, i2=i2):
                    ins = None
                    for h in range(4):
                        pr = h // 2
                        ins = e.matmul(pf[:, h * 128:(h + 1) * 128], lhsT=KE[:, d_, pr, tok], rhs=QEm[i2][:, d_, h, :], start=True, stop=True)
                    return ins
                S.op("pe", fa, [b_KE[i], b_QEm[i2]], [pa[d_][2]])
                mask = Mle if d_ == 0 else Mge
                S.op("dve", lambda e, d_=d_, mask=mask, pf=pa[d_][0], i2=i2: e.tensor_tensor(
                    out=AM[i2][:, d_], in0=pf.rearrange("p (h t) -> p h t", h=4), in1=mask[:].unsqueeze(1).to_broadcast([128, 4, 128]), op=ALU.mult),
                    [pa[d_][2], B_const], [b_AM[i2]])
            if i > 0:
                S.op("pool", lambda e, i=i, i2=i2: e.tensor_copy(out=kdb2[i2][:, 1, :], in_=KDB[:, i, :]), [b_KDB[i]], [b_kdb2[i2]])

        def p2back(i):
            tok = slice(i * 128, (i + 1) * 128)
            i2 = i % 2
            S.op("dve", lambda e: e.tensor_copy(out=SBb, in_=SB_), [b_SB], [b_SBb])
            po, _, pob = K.psum()

            def fo(e, po=po, i=i, i2=i2, tok=tok):
                ins = None
                for h in range(4):
                    pr = h // 2
                    o_ = po[:, h * 128:(h + 1) * 128]
                    e.matmul(o_, lhsT=AM[i2][:, 0, h, :], rhs=VG[:, i, h * 128:(h + 1) * 128], start=True, stop=False)
                    e.matmul(o_, lhsT=AM[i2][:, 1, h, :], rhs=VG[:, i, h * 128:(h + 1) * 128], start=False, stop=False)
                    e.matmul(o_, lhsT=QEm[i2][:, 0, h, :], rhs=SFP[:, i, pr, :], start=False, stop=False)
                    ins = e.matmul(o_, lhsT=QEm[i2][:, 1, h, :], rhs=SBb[:, pr, :], start=False, stop=True)
                return ins
            S.op("pe", fo, [b_AM[i2], b_VG[i], b_QEm[i2], b_SFP[i], b_SBb], [pob])
            st_, sb_ = smslot()
            S.op("act", lambda e, po=po: e.activation(out=sq, in_=po[:, 0:512], func=AF.Square), [pob], [b_sq])
            S.op("dve", lambda e, st_=st_: e.tensor_reduce(out=st_[:, 0:4], in_=sq.rearrange("p (h d) -> p h d", h=4), axis=AX.X, op=ALU.add), [b_sq], [sb_])
            S.op("dve", lambda e, st_=st_: e.tensor_scalar(out=st_[:, 0:4], in0=st_[:, 0:4], scalar1=1.0 / 128, scalar2=EPS, op0=ALU.mult, op1=ALU.add), [sb_], [sb_])
            S.op("pool", lambda e, st_=st_: e.tensor_tensor(out=st_[:, 0:4], in0=st_[:, 0:4], in1=smallc[:, 2:6], op=ALU.pow), [sb_, B_const], [sb_])
            S.op("dve", lambda e, st_=st_, po=po: e.tensor_tensor(out=o32.rearrange("p (h d) -> p h d", h=4), in0=po.rearrange("p (h d) -> p h d", h=4),
                                                                  in1=st_[:, 0:4].unsqueeze(2).to_broadcast([128, 4, 128]), op=ALU.mult), [pob, sb_], [b_o32])
            S.op("pool", lambda e: e.tensor_tensor(out=o32.rearrange("p (h d) -> p h d", h=4), in0=o32.rearrange("p (h d) -> p h d", h=4),
                                                   in1=glaw[:].unsqueeze(1).to_broadcast([128, 4, 128]), op=ALU.mult), [b_o32, B_const], [b_o32])
            S.op("dve", lambda e, i2=i2: e.tensor_tensor(out=ogl[i2], in0=o32, in1=ogs[i2], op=ALU.mult), [b_o32, b_ogs[i2]], [b_ogl[i2]])
            qb_, qi = divmod(i, NQS)
            sslot = qb_ % 2
            transpose4(ogl[i2], b_ogl[i2], stg[sslot][:, :, qi * 128:(qi + 1) * 128], b_stg[sslot])
            if qi == 0:
                half = (qb_ * QB) // HB
                S.dma("pool", MIXT_d[u, :, 0:4, qb_ * QB:(qb_ + 1) * QB], stg[sslot], [b_stg[sslot]], [b_MIX[u][half * 2 + 0]])
            if i > 0:
                ((pub, pubb),) = u_matmuls(kdb2[i2], b_kdb2[i2], VG[:, i, :], b_VG[i], (1,))
                for h in range(4):
                    pr, lo = h // 2, (h % 2) * 64
                    S.op("dve", lambda e, h=h, pr=pr, lo=lo, pub=pub, i=i: e.scalar_tensor_tensor(
                        out=SB_[lo:lo + 64, pr, :], in0=SB_[lo:lo + 64, pr, :], scalar=DECB[lo:lo + 64, i, pr:pr + 1],
                        in1=pub[lo:lo + 64, h * 128:(h + 1) * 128], op0=ALU.mult, op1=ALU.add), [pubb, b_DECB[i], b_SB, b_SBb], [b_SB])
        p2front(NT - 1)
        for i in range(NT - 1, -1, -1):
            if i > 0:
                p2front(i - 1)
            p2back(i)

    def phase_D(u, is_prompt, qcs, qsc):
        K.arena_reset()
        WD_ = K.ar([128, KC, 1536], BF16)
        b_WDi = Buf()
        cols = (0,) if is_prompt else (0, 512, 1024)
        for c0 in cols:
            S.dma("sp", WD_[:, :, c0:c0 + 512], WIN_d[:, :, 1568 + c0:1568 + c0 + 512], [b_W["WIN"]], [b_WDi])
        QT = K.ar([128, 4, UT], BF16)
        b_QT = bufs(NT)
        if not is_prompt:
            KT = K.ar([128, 4, UT], BF16)
            VP = K.ar([128, 4, NT, VW], BF16)
            b_KT, b_VP = Buf(), Buf()
            S.op("pool", lambda e: e.memset(VP[:], 1.0), (), [b_VP])
        else:
            KTst = [K.ar([128, UT], BF16) for _ in range(2)]
            VPst = [K.ar([128, NT, VW], BF16) for _ in range(2)]
            b_st = bufs(2)
        rotq = [K.ar([128, 512], BF16) for _ in range(2)]
        rotk = [K.ar([128, 512], BF16) for _ in range(2)]
        b_rotq, b_rotk = bufs(2), bufs(2)
        rtmp = [K.ar([128, 256], F32) for _ in range(2)]
        b_rtmp = bufs(2)

        def dfront(i):
            tok = slice(i * 128, (i + 1) * 128)
            hTi = hT[:, :, tok]
            r = i % 2
            pq, _, pqb = K.psum()
            S.op("pe", proj(pq, hTi, WD_, 0, 512), [b_hT[i], b_WDi], [pqb])
            rope(pq, pqb, rotq[r], b_rotq[r], qcs[:, i, :], qsc[:, i, :], rtmp[0], b_rtmp[0], 0.125)
            if not is_prompt:
                pk, _, pkb = K.psum()
                S.op("pe", proj(pk, hTi, WD_, 512, 512), [b_hT[i], b_WDi], [pkb])
                rope(pk, pkb, rotk[r], b_rotk[r], tabs[:, 2, i, :], tabs[:, 3, i, :], rtmp[1], b_rtmp[1], None)
                pv, _, pvb = K.psum()
                S.op("pe", proj(pv, hTi, WD_, 1024, 512), [b_hT[i], b_WDi], [pvb])
                copy_op(K, "dve", VP[:, :, i, 0:128], pv.rearrange("p (h d) -> p h d", h=4), [pvb], [b_VP])

        def dback(i):
            tok = slice(i * 128, (i + 1) * 128)
            r = i % 2
            transpose4(rotq[r], b_rotq[r], QT[:, :, tok], b_QT[i])
            if not is_prompt:
                transpose4(rotk[r], b_rotk[r], KT[:, :, tok], b_KT)
        dfront(0)
        for i in range(NT):
            if i + 1 < NT:
                dfront(i + 1)
            dback(i)
        accb = [5, 6, 7]
        K.ps_reserved = set(accb)
        b_accs = [K.psb[b_] for b_ in accb]
        o32 = K.ar([128, 128], F32)
        t1 = K.ar([128, 128], F32)
        b_o32, b_t1 = Buf(), Buf()
        OD = [K.ar([128, NQS, 128], BF16) for _ in range(2)]
        b_OD = bufs(2)
        ODT = [K.ar([128, QB], BF16) for _ in range(2)]
        b_ODT = bufs(2)
        nseg = cfg.NC if is_prompt else 1
        it = 0
        ld = 0

        def acc_ap(c, qs, n=129):
            a = c * NQS + qs
            return K.ps[accb[a // 3]][:, (a % 3) * 129:(a % 3) * 129 + n]
        PTL = [K.ar([128, QB], BF16) for _ in range(8)]
        b_PTL = bufs(8)
        ptl_i = [0]
        accS = [K.ar([128, 3, 512], F32) for _ in range(2)]
        b_accS = bufs(2)
        sq32 = K.ar([128, 128], F32)
        b_sq32 = Buf()
        LOOK = 1
        for h in range(4):
            for qb_ in range(UT // QB):
                qsl = slice(qb_ * QB, (qb_ + 1) * QB)
                qbufs = [b_QT[qb_ * NQS + j] for j in range(NQS)]

                def zf(e):
                    ins = None
                    for b_ in accb:
                        ins = e.matmul(K.ps[b_][:, 0:512], lhsT=zerosb[:, 0:128], rhs=zerosb[:, 0:512], start=True, stop=True)
                    return ins
                S.op("pe", zf, [B_const], b_accs)
                pend = []

                def emit_pv(item):
                    pts, vp_ap, kvb, kt, last = item

                    def fpv(e, pts=pts, vp_ap=vp_ap, kt=kt, last=last):
                        ins = None
                        for c in range(2):
                            for qs in range(NQS):
                                ins = e.matmul(acc_ap(c, qs), lhsT=pts[c][0][:, qs * 128:(qs + 1) * 128], rhs=vp_ap[:, kt, 0:129],
                                               start=False, stop=last, skip_group_check=True)
                        return ins
                    S.op("pe", fpv, [pts[0][1], pts[1][1]] + kvb, b_accs)
                for sg_ in range(nseg):
                    if is_prompt:
                        sl = ld % 2
                        ld += 1
                        S.dma("sp", KTst[sl], KT_d[sg_, h], [b_KV[sg_]], [b_st[sl]])
                        S.dma("sp", VPst[sl].rearrange("p a b -> p (a b)"), VP_d[sg_, h], [b_KV[sg_]], [b_st[sl]])
                        kt_ap, vp_ap, kvb = KTst[sl], VPst[sl], [b_st[sl]]
                    else:
                        kt_ap, vp_ap, kvb = KT[:, h, :], VP[:, h], [b_KT, b_VP]
                    for kt in range(NT):
                        last = (sg_ == nseg - 1 and kt == NT - 1)
                        pts = []
                        for c in range(2):
                            pf, _, pbuf = K.psum()
                            S.op("pe", lambda e, pf=pf, c=c, kt=kt, kt_ap=kt_ap, h=h, qsl=qsl: e.matmul(
                                pf[:, 0:QB], lhsT=kt_ap[c * 64:(c + 1) * 64, kt * 128:(kt + 1) * 128], rhs=QT[c * 64:(c + 1) * 64, h, qsl],
                                start=True, stop=True), kvb + qbufs, [pbuf])
                            pi_ = ptl_i[0]
                            ptl_i[0] = (pi_ + 1) % 8
                            pt_, ptb_ = PTL[pi_], b_PTL[pi_]
                            S.op("act", lambda e, pf=pf, pt_=pt_: e.activation(out=pt_, in_=pf[:, 0:QB], func=AF.Exp), [pbuf], [ptb_])
                            pts.append((pt_, ptb_))
                        pend.append((pts, vp_ap, kvb, kt, last))
                        if len(pend) > LOOK:
                            emit_pv(pend.pop(0))
                while pend:
                    emit_pv(pend.pop(0))
                o_i = it % 2
                it += 1
                aS = accS[o_i]
                for j_, b_ in enumerate(accb):
                    copy_op(K, "dve" if j_ != 1 else "act", aS[:, j_, :], K.ps[b_][:, 0:512], [K.psb[b_]], [b_accS[o_i]])

                def accs_ap(c, qs):
                    a_ = c * NQS + qs
                    return aS[:, a_ // 3, (a_ % 3) * 129:(a_ % 3) * 129 + 129]
                for qs in range(NQS):
                    st_, sb_ = smslot()
                    a0, a1 = accs_ap(0, qs), accs_ap(1, qs)
                    rb = [b_accS[o_i]]
                    S.op("dve", lambda e, st_=st_, a0=a0: e.reciprocal(out=st_[:, 0:1], in_=a0[:, 128:129]), rb, [sb_])
                    S.op("dve", lambda e, st_=st_, a1=a1: e.reciprocal(out=st_[:, 1:2], in_=a1[:, 128:129]), rb + [sb_], [sb_])
                    S.op("dve", lambda e, st_=st_, a1=a1: e.tensor_scalar(out=t1, in0=a1[:, 0:128], scalar1=st_[:, 1:2], scalar2=smallc[:, 0:1],
                                                                         op0=ALU.mult, op1=ALU.mult), rb + [sb_, B_const], [b_t1])
                    S.op("dve", lambda e, st_=st_, a0=a0: e.scalar_tensor_tensor(out=o32, in0=a0[:, 0:128], scalar=st_[:, 0:1], in1=t1,
                                                                                op0=ALU.mult, op1=ALU.add), rb + [sb_, b_t1], [b_o32])
                    S.op("dve", lambda e: e.tensor_tensor(out=sq32, in0=o32, in1=o32, op=ALU.mult), [b_o32], [b_sq32])
                    S.op("dve", lambda e, st_=st_: e.tensor_reduce(out=st_[:, 2:3], in_=sq32, axis=AX.X, op=ALU.add), [b_sq32, sb_], [sb_])
                    rstd_from(st_[:, 2:3], 128, st_[:, 3:4], [sb_], [sb_])
                    S.op("dve", lambda e, st_=st_, qs=qs, o_i=o_i: e.scalar_tensor_tensor(out=OD[o_i][:, qs, :], in0=o32, scalar=st_[:, 3:4], in1=subw[:],
                                                                                     op0=ALU.mult, op1=ALU.mult), [b_o32, sb_, B_const], [b_OD[o_i]])
                pf, pbf, pbuf = K.psum()

                def trd(e, pbf=pbf, o_i=o_i):
                    ins = None
                    for qs in range(NQS):
                        ins = e.transpose(out=pbf[:, qs * 128:(qs + 1) * 128], in_=OD[o_i][:, qs, :], identity=identb[:])
                    return ins
                S.op("pe", trd, [b_OD[o_i], B_const], [pbuf])
                copy_op(K, "dve", ODT[o_i], pbf[:, 0:QB], [pbuf], [b_ODT[o_i]])
                half = (qb_ * QB) // HB
                S.dma("pool", MIXT_d[u, :, 4 + h, qsl], ODT[o_i], [b_ODT[o_i]], [b_MIX[u][half * 2 + 1]])
        K.ps_reserved = set()

    def phase_M(msrc):
        K.arena_reset()
        Wkv = [K.ar([128, KC, 512], BF16) for _ in range(2)]
        b_Wkv = bufs(2)
        mT = K.ar([128, KC, MEM], BF16)
        b_mT = bufs(2)
        for mt in range(2):
            load_norm_T(msrc[mt * 128:(mt + 1) * 128, :], mT[:, :, mt * 128:(mt + 1) * 128], b_mT[mt])
        for cb in range(4):
            w_ = Wkv[cb % 2]
            S.dma("sp", w_, WXKV_d[:, :, cb * 512:(cb + 1) * 512], [b_W["WXKV"]], [b_Wkv[cb % 2]])
            if cb < 2:
                for j in range(4):
                    pf, _, pbuf = K.psum()

                    def fk(e, pf=pf, j=j, w_=w_):
                        ins = None
                        for c in range(KC):
                            ins = e.matmul(pf[:, 0:MEM], lhsT=w_[:, c, j * 128:(j + 1) * 128], rhs=mT[:, c, :], start=(c == 0), stop=(c == KC - 1))
                        return ins
                    S.op("pe", fk, [b_Wkv[cb % 2]] + b_mT, [pbuf])
                    copy_op(K, K.alt(), KMT[:, cb * 4 + j, :], pf[:, 0:MEM], [pbuf], [b_KM])
            else:
                for mt in range(2):
                    pf, _, pbuf = K.psum()

                    def fv(e, pf=pf, mt=mt, w_=w_):
                        ins = None
                        for c in range(KC):
                            ins = e.matmul(pf[:, 0:512], lhsT=mT[:, c, mt * 128:(mt + 1) * 128], rhs=w_[:, c, :], start=(c == 0), stop=(c == KC - 1))
                        return ins
                    S.op("pe", fv, [b_Wkv[cb % 2]] + b_mT, [pbuf])
                    hh = (cb - 2) * 2
                    copy_op(K, K.alt(), VMP[:, mt, hh:hh + 2, 0:256], pf[:, 0:512].rearrange("p (h d) -> p h d", h=2), [pbuf], [b_KM])

    pn_tmp = K.sb("pn_tmp", [128, 2, 512], F32)
    b_pn = bufs(2)
    pn_i = [0]

    def post_norm_res(pbanks, pbufs, xres, xres_b, out_ap, out_b, wk):
        st_, sb_ = smslot()
        for j in range(2):
            S.op("act", lambda e, j=j, st_=st_: e.activation(out=junk[:, 0:512], in_=pbanks[j][:, 0:512], func=AF.Square, accum_out=st_[:, j:j + 1]),
                 [pbufs[j], sb_], [b_junk, sb_])
        S.op("dve", lambda e, st_=st_: e.tensor_tensor(out=st_[:, 2:3], in0=st_[:, 0:1], in1=st_[:, 1:2], op=ALU.add), [sb_], [sb_])
        rstd_from(st_[:, 2:3], D, st_[:, 3:4], [sb_], [sb_])
        for j in range(2):
            sl = slice(j * 512, (j + 1) * 512)
            k = pn_i[0] % 2
            pn_i[0] += 1
            S.op("dve", lambda e, j=j, sl=sl, st_=st_, k=k: e.scalar_tensor_tensor(out=pn_tmp[:, k, :], in0=pbanks[j][:, 0:512], scalar=st_[:, 3:4], in1=wpost[:, wk, sl],
                                                                                 op0=ALU.mult, op1=ALU.mult), [pbufs[j], sb_, B_const], [b_pn[k]])
            S.op("pool", lambda e, sl=sl, k=k: e.tensor_tensor(out=out_ap[:, sl], in0=pn_tmp[:, k, :], in1=xres[:, sl], op=ALU.add), [b_pn[k], xres_b], [out_b])

    def phase_T(u, half, xsrc, ydst):
        K.arena_reset()
        HT_ = HB // 128
        t0 = half * HT_
        X1 = K.ar([128, HT_, D], F32)
        b_X1 = bufs(HT_)
        hTh = hT[:, :, 0:HB]
        b_hTh = bufs(HT_)
        mark = K.arena_off
        MIXh = K.ar([128, KC, HB], BF16)
        b_MIXh = Buf()
        S.dma("sp", MIXh[:, 0:4, :], MIXT_d[u, :, 0:4, half * HB:(half + 1) * HB], [b_MIX[u][half * 2 + 0]], [b_MIXh])
        S.dma("sp", MIXh[:, 4:8, :], MIXT_d[u, :, 4:8, half * HB:(half + 1) * HB], [b_MIX[u][half * 2 + 1]], [b_MIXh])
        Wa = K.ar([128, KC, D], BF16)
        Wb = K.ar([128, KC, D], BF16)
        b_Wa, b_Wb = Buf(), Buf()
        S.dma("sp", Wa, WOUT_d[:, :, :], [b_W["WOUT"]], [b_Wa])
        S.dma("sp", Wb, WXQ_d[:, :, :], [b_W["WXQ"]], [b_Wb])
        XQT = K.ar([128, KC, QB], BF16)
        b_XQT = Buf()
        OX = K.ar([128, NQS, D], BF16)
        b_OX = bufs(NQS)
        PTE = [K.ar([128, QB], BF16) for _ in range(8)]
        b_PTE = bufs(8)
        pte_i = [0]
        for tl in range(HT_):
            tg = t0 + tl
            tok = slice(tl * 128, (tl + 1) * 128)
            xa, xb_ = xslot()
            S.dma("sp", xa, xsrc[tg * 128:(tg + 1) * 128, :], (), [xb_])
            pp = [K.psum() for _ in range(2)]
            for nb in range(2):
                def fo(e, nb=nb, tok=tok, pf=pp[nb][0]):
                    ins = None
                    for c in range(KC):
                        ins = e.matmul(pf[:, 0:512], lhsT=MIXh[:, c, tok], rhs=Wa[:, c, nb * 512:(nb + 1) * 512], start=(c == 0), stop=(c == KC - 1))
                    return ins
                S.op("pe", fo, [b_MIXh, b_Wa], [pp[nb][2]])
            post_norm_res([pp[0][0], pp[1][0]], [pp[0][2], pp[1][2]], xa, xb_, X1[:, tl, :], b_X1[tl], 0)
        S.dma("sp", Wa, WXO_d[:, :, :], [b_W["WXO"]], [b_Wa])
        for tl in range(HT_):
            norm_T(X1[:, tl, :], b_X1[tl], hTh[:, :, tl * 128:(tl + 1) * 128], b_hTh[tl])
        for qb_ in range(HB // QB):
            tls = [qb_ * NQS + j for j in range(NQS)]
            qsl = slice(qb_ * QB, (qb_ + 1) * QB)
            for cc in range(KC):
                pf, _, pbuf = K.psum()

                def fq(e, pf=pf, cc=cc, qsl=qsl):
                    ins = None
                    for c in range(KC):
                        ins = e.matmul(pf[:, 0:QB], lhsT=Wb[:, c, cc * 128:(cc + 1) * 128], rhs=hTh[:, c, qsl], start=(c == 0), stop=(c == KC - 1))
                    return ins
                S.op("pe", fq, [b_Wb] + [b_hTh[t] for t in tls], [pbuf])
                copy_op(K, K.alt(), XQT[:, cc, :], pf[:, 0:QB], [pbuf], [b_XQT])
            allpts = []
            for h in range(4):
                pts = []
                for mt in range(2):
                    pf, _, pbuf = K.psum()

                    def fs(e, pf=pf, mt=mt, h=h):
                        e.matmul(pf[:, 0:QB], lhsT=KMT[:, 2 * h, mt * 128:(mt + 1) * 128], rhs=XQT[:, 2 * h, :], start=True, stop=False)
                        return e.matmul(pf[:, 0:QB], lhsT=KMT[:, 2 * h + 1, mt * 128:(mt + 1) * 128], rhs=XQT[:, 2 * h + 1, :], start=False, stop=True)
                    S.op("pe", fs, [b_KM, b_XQT], [pbuf])
                    pe_i = pte_i[0]
                    pte_i[0] = (pe_i + 1) % 8
                    pt_, ptb_ = PTE[pe_i], b_PTE[pe_i]
                    S.op("act", lambda e, pf=pf, pt_=pt_: e.activation(out=pt_, in_=pf[:, 0:QB], func=AF.Exp, scale=1.0 / 16), [pbuf], [ptb_])
                    pts.append((pt_, ptb_))
                allpts.append(pts)
            for h in range(4):
                pts = allpts[h]
                for qs in range(NQS):
                    pf, _, pbuf = K.psum()

                    def fpv(e, pf=pf, qs=qs, h=h, pts=pts):
                        e.matmul(pf[:, 0:257], lhsT=pts[0][0][:, qs * 128:(qs + 1) * 128], rhs=VMP[:, 0, h, 0:257], start=True, stop=False)
                        return e.matmul(pf[:, 0:257], lhsT=pts[1][0][:, qs * 128:(qs + 1) * 128], rhs=VMP[:, 1, h, 0:257], start=False, stop=True)
                    S.op("pe", fpv, [pts[0][1], pts[1][1], b_KM], [pbuf])
                    st_, sb_ = smslot()
                    S.op("dve", lambda e, st_=st_, pf=pf: e.reciprocal(out=st_[:, 0:1], in_=pf[:, 256:257]), [pbuf], [sb_])
                    S.op("dve", lambda e, st_=st_, pf=pf, qs=qs, h=h: e.tensor_scalar(out=OX[:, qs, h * 256:(h + 1) * 256], in0=pf[:, 0:256], scalar1=st_[:, 0:1],
                                                                                   scalar2=None, op0=ALU.mult), [pbuf, sb_], [b_OX[qs]])
            for j, tl in enumerate(tls):
                tok = slice(tl * 128, (tl + 1) * 128)
                pf, pbf, pbuf = K.psum()

                def trx(e, pbf=pbf, j=j):
                    ins = None
                    for c in range(KC):
                        ins = e.transpose(out=pbf[:, c * 128:(c + 1) * 128], in_=OX[:, j, c * 128:(c + 1) * 128], identity=identb[:])
                    return ins
                S.op("pe", trx, [b_OX[j], B_const], [pbuf])
                copy_op(K, K.alt(), hTh[:, :, tok], pbf.rearrange("p (a b) -> p a b", a=KC), [pbuf, b_XQT], [b_hTh[tl]])
                pp = [K.psum() for _ in range(2)]
                for nb in range(2):
                    def fx(e, nb=nb, tok=tok, pf=pp[nb][0]):
                        ins = None
                        for c in range(KC):
                            ins = e.matmul(pf[:, 0:512], lhsT=hTh[:, c, tok], rhs=Wa[:, c, nb * 512:(nb + 1) * 512], start=(c == 0), stop=(c == KC - 1))
                        return ins
                    S.op("pe", fx, [b_hTh[tl], b_Wa], [pp[nb][2]])
                post_norm_res([pp[0][0], pp[1][0]], [pp[0][2], pp[1][2]], X1[:, tl, :], b_X1[tl], X1[:, tl, :], b_X1[tl], 1)
        S.barrier()
        K.arena_off = mark
        HID = K.ar([128, FC, HB], BF16)
        b_HID = bufs(FC)
        GU = [K.ar([128, 2, KC, 128], BF16) for _ in range(3)]
        b_GU = bufs(3)
        WDr = [K.ar([128, 512], BF16) for _ in range(6)]
        b_WDr = bufs(6)
        YR = K.ar([128, 4, D], F32)
        b_YR = bufs(4)
        sgl = [K.ar([128, 512], F32) for _ in range(1)]
        b_sgl = bufs(1)
        for tl in range(HT_):
            norm_T(X1[:, tl, :], b_X1[tl], hTh[:, :, tl * 128:(tl + 1) * 128], b_hTh[tl])
        k_ = 0
        for fc in range(FC):
            g_ = fc % 3
            S.dma("sp", GU[g_].rearrange("p a b c -> p (a b c)"), WGU_d[fc].rearrange("p a b c -> p (a b c)"), [b_W["WGU"]], [b_GU[g_]])
            for sb_i in range(HB // 512):
                tsl = slice(sb_i * 512, (sb_i + 1) * 512)
                tl4 = [sb_i * 4 + j for j in range(4)]
                pg = K.psum()
                pu = K.psum()
                for which, pz in ((0, pg), (1, pu)):
                    def fg(e, which=which, pf=pz[0], g_=g_, tsl=tsl):
                        ins = None
                        for c in range(KC):
                            ins = e.matmul(pf[:, 0:512], lhsT=GU[g_][:, which, c, :], rhs=hTh[:, c, tsl], start=(c == 0), stop=(c == KC - 1))
                        return ins
                    S.op("pe", fg, [b_GU[g_]] + [b_hTh[t] for t in tl4], [pz[2]])
                s_i = 0
                k_ += 1
                S.op("act", lambda e, s_i=s_i, pf=pg[0]: e.activation(out=sgl[s_i], in_=pf[:, 0:512], func=AF.Silu), [pg[2]], [b_sgl[s_i]])
                S.op("dve", lambda e, s_i=s_i, pf=pu[0], fc=fc, tsl=tsl: e.tensor_tensor(out=HID[:, fc, tsl], in0=pf[:, 0:512], in1=sgl[s_i], op=ALU.mult),
                     [pu[2], b_sgl[s_i]], [b_HID[fc]])
        wd_i = 0
        for grp in range(HT_ // 4):
            tls = [grp * 4 + j for j in range(4)]
            for nb in range(2):
                acc = [K.psum() for _ in range(4)]
                for fc in range(FC):
                    w_i = wd_i % 6
                    wd_i += 1
                    S.dma("sp", WDr[w_i], WD_d[fc, :, nb * 512:(nb + 1) * 512], [b_W["WD"]], [b_WDr[w_i]])
                    for j, tl in enumerate(tls):
                        S.op("pe", lambda e, j=j, tl=tl, fc=fc, w_i=w_i, pf=acc[j][0]: e.matmul(
                            pf[:, 0:512], lhsT=HID[:, fc, tl * 128:(tl + 1) * 128], rhs=WDr[w_i], start=(fc == 0), stop=(fc == FC - 1)),
                            [b_HID[fc], b_WDr[w_i]], [acc[j][2]])
                for j, tl in enumerate(tls):
                    st_ = sm[:, 48 + j * 4:52 + j * 4]
                    S.op("act", lambda e, j=j, nb=nb, st_=st_, pf=acc[j][0]: e.activation(out=YR[:, j, nb * 512:(nb + 1) * 512], in_=pf[:, 0:512], func=AF.Copy),
                         [acc[j][2]], [b_YR[j]])
                    S.op("act", lambda e, j=j, nb=nb, st_=st_: e.activation(out=junk[:, 0:512], in_=YR[:, j, nb * 512:(nb + 1) * 512], func=AF.Square,
                                                                           accum_out=st_[:, nb:nb + 1]), [b_YR[j], b_ysm[j]], [b_junk, b_ysm[j]])
            for j, tl in enumerate(tls):
                tg = t0 + tl
                st_ = sm[:, 48 + j * 4:52 + j * 4]
                S.op("dve", lambda e, st_=st_: e.tensor_tensor(out=st_[:, 2:3], in0=st_[:, 0:1], in1=st_[:, 1:2], op=ALU.add), [b_ysm[j]], [b_ysm[j]])
                rstd_from(st_[:, 2:3], D, st_[:, 3:4], [b_ysm[j]], [b_ysm[j]])
                for nb in range(2):
                    sl = slice(nb * 512, (nb + 1) * 512)
                    S.op("dve", lambda e, j=j, sl=sl, st_=st_: e.scalar_tensor_tensor(out=YR[:, j, sl], in0=YR[:, j, sl], scalar=st_[:, 3:4], in1=wpost[:, 2, sl],
                                                                                    op0=ALU.mult, op1=ALU.mult), [b_YR[j], b_ysm[j], B_const], [b_YR[j]])
                    S.op("pool", lambda e, j=j, sl=sl, tl=tl: e.tensor_tensor(out=YR[:, j, sl], in0=YR[:, j, sl], in1=X1[:, tl, sl], op=ALU.add),
                         [b_YR[j], b_X1[tl]], [b_YR[j]])
                S.dma("pool", ydst[tg * 128:(tg + 1) * 128, :], YR[:, j, :], [b_YR[j]], [b_yout])

    b_ysm = bufs(4)
    b_yout = Buf()

    import os as _os
    STOP = _os.environ.get("KSTOP", "")

    def finish():
        S.final_wait()
        S.emit(block, sems)
        st.close()
        return nc, K
    setup_consts()
    if STOP == "consts":
        return finish()
    K.arena_reset()
    wk1 = K.ar([128, 4, NT, 8], F32)
    make_tables(NT, False, tabs[:, 0], tabs[:, 1], 0.125, wk1)
    make_tables(NT, False, tabs[:, 2], tabs[:, 3], 1.0, wk1)
    make_tables(NT, True, tabs[:, 4], tabs[:, 5], 0.125, wk1)
    S.barrier()
    if STOP == "tables":
        return finish()
    weight_prep()
    S.barrier()
    if STOP == "wprep":
        return finish()
    pre_phase()
    S.barrier()
    if STOP == "pre":
        return finish()
    for u in range(NU):
        if u == 0:
            xsrc, msrc, ydst = xps, memp, yp
        else:
            xsrc = xs[(u - 1) * UT:u * UT, :]
            msrc = mems[(u - 1) * MEM:u * MEM, :]
            ydst = ys[(u - 1) * UT:u * UT, :]
        phase_G(u, xsrc)
        S.barrier()
        if STOP in ("G%d" % u, "G%da" % u):
            return finish()
        if u == 0:
            phase_D(u, True, tabs[:, 4], tabs[:, 5])
        else:
            phase_D(u, False, tabs[:, 0], tabs[:, 1])
        S.barrier()
        if STOP == "D%d" % u:
            return finish()
        phase_M(msrc)
        S.barrier()
        if STOP == "M%d" % u:
            return finish()
        for half in range(UT // HB):
            phase_T(u, half, xsrc, ydst)
            S.barrier()
    S.final_wait()
    S.emit(block, sems)
    st.close()
    return nc, K


_CACHE = {}


def _inputs_for_core(cfg, c, I):
    UT, NSQ, T, NT = cfg.UT, cfg.NSQ, cfg.T, cfg.NT
    m = {}
    xp = np.ascontiguousarray(I["x_prompt"][0])
    m["xpf"] = xp
    m["xps"] = np.ascontiguousarray(xp[c * UT:(c + 1) * UT])
    m["xs"] = np.ascontiguousarray(I["x_sample"][c * NSQ:(c + 1) * NSQ].reshape(NSQ * UT, D))
    m["memp"] = np.ascontiguousarray(I["mem_prompt"][0])
    m["mems"] = np.ascontiguousarray(I["mem_sample"][c * NSQ:(c + 1) * NSQ].reshape(NSQ * MEM, D))
    m["posb"] = np.full((128, 1), float(c * UT), np.float32)
    mk = np.zeros((128, 2, T), np.float32)
    mk[:, 0, :c * NT] = 1.0
    mk[:, 1, (c + 1) * NT:] = 1.0
    m["mk"] = mk.reshape(128, 2 * T)
    for n, k in [("n_mix_pre", "norm_mix_pre"), ("n_mix_post", "norm_mix_post"), ("n_x_pre", "norm_xattn_pre"), ("n_mem", "norm_mem"),
                 ("n_x_post", "norm_xattn_post"), ("n_f_pre", "norm_ffn_pre"), ("n_f_post", "norm_ffn_post"), ("bg_f", "b_gate_f"),
                 ("bg_b", "b_gate_b"), ("gla_nw", "gla_norm_w"), ("lq1", "lambda_q1"), ("lk1", "lambda_k1"), ("lq2", "lambda_q2"),
                 ("lk2", "lambda_k2"), ("sub_w", "diff_subln_w")]:
        m[n] = np.ascontiguousarray(I[k][0:1]).astype(np.float32)
    for n, k in [("w_in", "w_in"), ("wgu_f", "w_gate_up_f"), ("wgu_b", "w_gate_up_b"), ("w_out", "w_out"), ("w_xq", "w_xq"), ("w_xkv", "w_xkv"),
                 ("w_xo", "w_xo"), ("w_fg", "w_ffn_gate"), ("w_fu", "w_ffn_up"), ("w_fd", "w_ffn_down")]:
        m[n] = np.ascontiguousarray(I[k][0]).astype(np.float32)
    return m


def run(cfg, I, debug=False):
    key = (cfg.UT, cfg.NSQ, debug)
    if key not in _CACHE:
        _CACHE[key] = build(cfg, debug)
    nc, K = _CACHE[key]
    in_maps = [_inputs_for_core(cfg, c, I) for c in range(cfg.NC)]
    res = run_bass_kernel_spmd(nc, in_maps, core_ids=list(range(cfg.NC)))
    return res


def kernel(**inputs):
    cfg = Cfg(UT=2048, NSQ=4, NC=8)
    I = {k: np.asarray(v) for k, v in inputs.items()}
    res = run(cfg, I)
    yp = np.concatenate([r["yp"] for r in res.results], axis=0)[None]
    ys = np.concatenate([r["ys"].reshape(cfg.NSQ, cfg.UT, D) for r in res.results], axis=0)
    return (yp.astype(np.float32), ys.astype(np.float32))
```

```python
import math
import contextlib
import numpy as np
import concourse.bass as bass
import concourse.mybir as mybir
from concourse.bass_utils import run_bass_kernel_spmd

F32 = mybir.dt.float32
BF16 = mybir.dt.bfloat16
I32 = mybir.dt.int32
AF = mybir.ActivationFunctionType
ALU = mybir.AluOpType
AX = mybir.AxisListType

D = 1024
KC = 8
MEM = 256
DFF = 2816
FC = 22
INC = 3104
EPS = 1e-6
LAM_INIT = 0.2
VW = 130
THETA = 500000.0


class Cfg:
    def __init__(self, UT=2048, NSQ=4, NC=8):
        self.UT = UT
        self.NT = UT // 128
        self.NSQ = NSQ
        self.NC = NC
        self.T = NC * self.NT
        self.QB = min(512, UT)
        self.NQS = self.QB // 128
        self.HB = min(1024, UT)


class Buf:
    __slots__ = ("lw", "rd")

    def __init__(self):
        self.lw = None
        self.rd = {}


def bufs(n):
    return [Buf() for _ in range(n)]


class Sched:
    ENG = ["pe", "dve", "act", "pool", "sp"]

    def __init__(self, nc, nd_sp=28, nd_pool=12):
        self.nc = nc
        self.q = {e: [] for e in self.ENG}
        self.cnt = {e: 0 for e in self.ENG}
        self.waited = {e: {} for e in self.ENG}
        self.dpool = {"sp": ["ds%d" % k for k in range(nd_sp)], "pool": ["dp%d" % k for k in range(nd_pool)]}
        self.dnext = {"sp": 0, "pool": 0}
        for names in self.dpool.values():
            for n in names:
                self.cnt[n] = 0
        self.n_ins = 0

    def _deps(self, eng, reads, writes):
        deps = {}
        for b in reads:
            if b.lw is not None:
                e, i = b.lw
                if deps.get(e, 0) < i:
                    deps[e] = i
        for b in writes:
            if b.lw is not None:
                e, i = b.lw
                if deps.get(e, 0) < i:
                    deps[e] = i
            for e, i in b.rd.items():
                if deps.get(e, 0) < i:
                    deps[e] = i
        waits = []
        w = self.waited[eng]
        for e, i in deps.items():
            if e == eng and eng == "pe":
                continue
            if w.get(e, 0) < i:
                w[e] = i
                waits.append((e, i))
        return waits

    def _mark(self, me, reads, writes):
        e, i = me
        for b in reads:
            if b.rd.get(e, 0) < i:
                b.rd[e] = i
        for b in writes:
            b.lw = me
            b.rd = {}

    def op(self, eng, fn, reads=(), writes=()):
        waits = self._deps(eng, reads, writes)
        self.cnt[eng] += 1
        me = (eng, self.cnt[eng])
        self._mark(me, reads, writes)
        self.q[eng].append((waits, fn, (eng, 1)))

    def dma(self, qeng, out, in_, reads=(), writes=(), **kw):
        k = self.dnext[qeng]
        names = self.dpool[qeng]
        self.dnext[qeng] = (k + 1) % len(names)
        sn = names[k]
        waits = self._deps(qeng, reads, writes)
        prev = self.cnt[sn]
        if prev > 0 and self.waited[qeng].get(sn, 0) < prev:
            self.waited[qeng][sn] = prev
            waits.append((sn, prev))
        self.cnt[sn] += 1
        me = (sn, self.cnt[sn])
        self._mark(me, reads, writes)
        self.q[qeng].append((waits, lambda e: e.dma_start(out=out, in_=in_, **kw), (sn, 16)))

    def barrier(self):
        engs = ("pe", "dve", "act", "pool")
        for e in engs:
            waits = []
            for e2, c in self.cnt.items():
                if e2 == e or c == 0 or e2 == "sp":
                    continue
                if self.waited[e].get(e2, 0) < c:
                    self.waited[e][e2] = c
                    waits.append((e2, c))
            if waits:
                self.q[e].append((waits, None, None))
        waits = []
        for e2, c in self.cnt.items():
            if e2 == "sp" or c == 0:
                continue
            if self.waited["sp"].get(e2, 0) < c:
                self.waited["sp"][e2] = c
                waits.append((e2, c))
        if waits:
            self.q["sp"].append((waits, None, None))

    def final_wait(self):
        for eng in ("sp",):
            waits = [(e2, c) for e2, c in self.cnt.items() if e2 != eng and c > 0]
            self.q[eng].append((waits, None, None))

    def emit(self, block, sems):
        def run(ename, e):
            n = 0
            for waits, fn, inc in self.q[ename]:
                for sn, v in waits:
                    e.wait_ge(sems[sn], v * (16 if sn[0] == "d" and sn != "dve" else 1))
                if fn is not None:
                    ins = fn(e)
                    if inc is not None:
                        ins.then_inc(sems[inc[0]], inc[1])
                    n += 1
            self.n_ins += n

        @block.tensor
        def _(e):
            run("pe", e)

        @block.vector
        def _(e):
            run("dve", e)

        @block.scalar
        def _(e):
            run("act", e)

        @block.gpsimd
        def _(e):
            run("pool", e)

        @block.sync
        def _(e):
            run("sp", e)


class KB:
    def __init__(self, cfg, debug=False):
        self.cfg = cfg
        self.debug = debug
        self.nc = bass.Bass("TRN2", target_bir_lowering=False)
        self.S = Sched(self.nc)
        self.st = contextlib.ExitStack()
        self.dr = {}
        self.arena_off = 0
        self.tog = 0

    def din(self, name, shape, dt=F32):
        t = self.nc.dram_tensor(name, list(shape), dt, kind="ExternalInput").ap()
        self.dr[name] = t
        return t

    def dout(self, name, shape, dt=F32):
        t = self.nc.dram_tensor(name, list(shape), dt, kind="ExternalOutput").ap()
        self.dr[name] = t
        return t

    def dscr(self, name, shape, dt=BF16, dbg=False):
        kind = "ExternalOutput" if (dbg and self.debug) else "Internal"
        t = self.nc.dram_tensor(name, list(shape), dt, kind=kind).ap()
        self.dr[name] = t
        return t

    def sb(self, name, shape, dt):
        return self.st.enter_context(self.nc.sbuf_tensor(name, list(shape), dt))

    def arena_reset(self):
        self.arena_off = 0

    def ar(self, shape, dt):
        n = 1
        for s in shape[1:]:
            n *= s
        nb = n * (2 if dt == F32 else 1)
        nb = (nb + 1) // 2 * 2
        off = self.arena_off
        assert off + nb <= self.ARENA, "arena overflow %d + %d > %d" % (off, nb, self.ARENA)
        self.arena_off = off + nb
        v = self.arena[:, off:off + nb]
        if dt == F32:
            v = v.bitcast(F32)
        v = v[:, 0:n]
        if len(shape) == 3:
            v = v.rearrange("p (a b) -> p a b", a=shape[1])
        elif len(shape) == 4:
            v = v.rearrange("p (a b c) -> p a b c", a=shape[1], b=shape[2])
        return v

    def psum(self):
        while True:
            i = self.ps_next
            self.ps_next = (i + 1) % 8
            if i not in self.ps_reserved:
                break
        return self.ps[i][:], self.ps[i][:].bitcast(BF16), self.psb[i]

    def alt(self):
        self.tog ^= 1
        return "dve" if self.tog else "act"


def copy_op(K, eng, out, in_, reads, writes, scale=None):
    if eng == "act":
        if scale is None:
            K.S.op("act", lambda e: e.activation(out=out, in_=in_, func=AF.Copy), reads, writes)
        else:
            K.S.op("act", lambda e: e.activation(out=out, in_=in_, func=AF.Copy, scale=scale), reads, writes)
    else:
        if scale is None:
            K.S.op(eng, lambda e: e.tensor_copy(out=out, in_=in_), reads, writes)
        else:
            K.S.op(eng, lambda e: e.tensor_scalar(out=out, in0=in_, scalar1=scale, scalar2=None, op0=ALU.mult), reads, writes)


def build(cfg, debug=False):
    K = KB(cfg, debug)
    nc, S, st = K.nc, K.S, K.st
    UT, NT, NSQ, T, QB, NQS, HB = cfg.UT, cfg.NT, cfg.NSQ, cfg.T, cfg.QB, cfg.NQS, cfg.HB
    NU = 1 + NSQ

    xpf = K.din("xpf", [T * 128, D])
    xps = K.din("xps", [UT, D])
    xs = K.din("xs", [NSQ * UT, D])
    memp = K.din("memp", [MEM, D])
    mems = K.din("mems", [NSQ * MEM, D])
    posb_d = K.din("posb", [128, 1])
    mk_d = K.din("mk", [128, 2 * T])
    vecs = {}
    for n, ln in [("n_mix_pre", D), ("n_mix_post", D), ("n_x_pre", D), ("n_mem", D), ("n_x_post", D),
                  ("n_f_pre", D), ("n_f_post", D), ("bg_f", 256), ("bg_b", 256), ("gla_nw", 128),
                  ("lq1", 64), ("lk1", 64), ("lq2", 64), ("lk2", 64), ("sub_w", 128)]:
        vecs[n] = K.din(n, [1, ln])
    w_in = K.din("w_in", [D, INC])
    wgu_f = K.din("wgu_f", [16, 256])
    wgu_b = K.din("wgu_b", [16, 256])
    w_out = K.din("w_out", [D, D])
    w_xq = K.din("w_xq", [D, D])
    w_xkv = K.din("w_xkv", [D, 2 * D])
    w_xo = K.din("w_xo", [D, D])
    w_fg = K.din("w_fg", [D, DFF])
    w_fu = K.din("w_fu", [D, DFF])
    w_fd = K.din("w_fd", [DFF, D])
    yp = K.dout("yp", [UT, D])
    ys = K.dout("ys", [NSQ * UT, D])
    WIN_d = K.dscr("WIN_d", [128, KC, INC])
    WOUT_d = K.dscr("WOUT_d", [128, KC, D])
    WXQ_d = K.dscr("WXQ_d", [128, KC, D])
    WXKV_d = K.dscr("WXKV_d", [128, KC, 2 * D])
    WXO_d = K.dscr("WXO_d", [128, KC, D])
    WGU_d = K.dscr("WGU_d", [FC, 128, 2, KC, 128])
    WD_d = K.dscr("WD_d", [FC, 128, D])
    KT_d = K.dscr("KT_d", [cfg.NC, 4, 128, UT], dbg=True)
    VP_d = K.dscr("VP_d", [cfg.NC, 4, 128, NT * VW], dbg=True)
    MIXT_d = K.dscr("MIXT_d", [NU, 128, KC, UT], dbg=True)
    if debug:
        SDBG_d = K.dout("SDBG_d", [128, 2, 2, 128])

    sems = {n: st.enter_context(nc.semaphore("s_" + n)) for n in list(S.cnt.keys())}
    identb = K.sb("identb", [128, 128], BF16)
    identf = K.sb("identf", [128, 128], F32)
    onesf = K.sb("onesf", [128, 128], F32)
    Tle = K.sb("Tle", [128, 128], F32)
    Tge = K.sb("Tge", [128, 128], F32)
    Tgt = K.sb("Tgt", [128, 128], F32)
    Tlt = K.sb("Tlt", [128, 128], F32)
    Mle = K.sb("Mle", [128, 128], F32)
    Mge = K.sb("Mge", [128, 128], F32)
    zerosb = K.sb("zerosb", [128, 512], BF16)
    wpre = K.sb("wpre", [128, 4, KC], F32)
    wpost = K.sb("wpost", [128, 3, D], F32)
    glaw = K.sb("glaw", [128, 128], F32)
    subw = K.sb("subw", [128, 128], F32)
    wup = K.sb("wup", [128, 512], F32)
    hm = K.sb("hm", [128, 2], F32)
    smallc = K.sb("smallc", [128, 16], F32)
    mkt = K.sb("mkt", [128, 2, T], F32)
    omk = K.sb("omk", [128, 2, T], F32)
    posb = K.sb("posbt", [128, 1], F32)
    hT = K.sb("hT", [128, KC, UT], BF16)
    KMT = K.sb("KMT", [128, KC, MEM], BF16)
    VMP = K.sb("VMP", [128, 2, 4, 258], BF16)
    xst = K.sb("xst", [128, 2, D], F32)
    xnb = K.sb("xnb", [128, 2, D], BF16)
    junk = K.sb("junk", [128, D], BF16)
    sm = K.sb("sm", [128, 64], F32)
    PT = K.sb("PT", [128, 4, QB], BF16)
    tabs = K.sb("tabs", [128, 6, NT, 16], F32)
    K.ARENA = 57400
    K.arena = K.sb("arena", [128, K.ARENA], BF16)
    K.ps = [st.enter_context(nc.psum_tensor("ps%d" % i, [128, 512], F32)) for i in range(8)]
    K.psb = bufs(8)
    K.ps_next = 0
    K.ps_reserved = set()
    block = st.enter_context(nc.Block())

    B_const = Buf()
    b_xst = bufs(2)
    b_xnb = bufs(2)
    b_junk = Buf()
    b_PT = bufs(4)
    b_hT = bufs(NT)
    b_KM = Buf()
    sm_i = [0]
    b_sm = bufs(12)

    def smslot():
        i = sm_i[0]
        sm_i[0] = (i + 1) % 12
        return sm[:, i * 4:(i + 1) * 4], b_sm[i]

    xi = [0]

    def xslot():
        i = xi[0]
        xi[0] = (i + 1) % 2
        return xst[:, i, :], b_xst[i]

    xni = [0]

    def xnslot():
        i = xni[0]
        xni[0] = (i + 1) % 2
        return xnb[:, i, :], b_xnb[i]

    pti = [0]

    def ptslot():
        i = pti[0]
        pti[0] = (i + 1) % 4
        return PT[:, i, :], b_PT[i]

    def setup_consts():
        S.op("pool", lambda e: e.memset(onesf[:], 1.0), (), [B_const])
        S.op("pool", lambda e: e.memset(zerosb[:], 0.0), (), [B_const])
        S.op("pool", lambda e: e.memset(smallc[:], 0.0), (), [B_const])
        S.op("pool", lambda e: e.memset(sm[:], 0.0), (), [B_const])
        S.op("pool", lambda e: e.memset(smallc[:, 1:2], -1.0 / 16), [B_const], [B_const])
        S.op("pool", lambda e: e.memset(smallc[:, 2:6], -0.5), [B_const], [B_const])
        S.op("pool", lambda e: e.memset(wup[:], 0.0), (), [B_const])
        S.op("pool", lambda e: e.memset(hm[:], 0.0), (), [B_const])
        S.op("pool", lambda e: e.memset(hm[0:64, 0:1], 1.0), [B_const], [B_const])
        S.op("pool", lambda e: e.memset(hm[64:128, 1:2], 1.0), [B_const], [B_const])
        S.op("pool", lambda e: e.memset(VMP[:], 1.0), (), [b_KM])

        def asel(out, pat, cm, op, fill=0.0):
            S.op("pool", lambda e: e.affine_select(out=out, in_=onesf[:], pattern=[[pat, 128]], compare_op=op,
                                                    fill=fill, base=0, channel_multiplier=cm), [B_const], [B_const])
        asel(identf[:], 1, -1, ALU.is_equal)
        asel(Mle[:], 1, -1, ALU.is_ge)
        asel(Mge[:], -1, 1, ALU.is_ge)
        asel(Tgt[:], -1, 1, ALU.is_gt)
        asel(Tlt[:], 1, -1, ALU.is_gt)
        S.op("pool", lambda e: e.tensor_copy(out=identb[:], in_=identf[:]), [B_const], [B_const])
        for dst, src in ((Tle, Mle), (Tge, Mge), (Tgt, Tgt), (Tlt, Tlt)):
            S.op("pool", lambda e, dst=dst, src=src: e.tensor_scalar(out=dst[:], in0=src[:], scalar1=-1.0 / 16, scalar2=None,
                                                                     op0=ALU.mult), [B_const], [B_const])
        for k, n in enumerate(["n_mix_pre", "n_x_pre", "n_f_pre", "n_mem"]):
            S.dma("sp", wpre[:, k, :], vecs[n].rearrange("o (c p) -> p (o c)", p=128), (), [B_const],
                  allow_slow_non_contiguous=True)
        for k, n in enumerate(["n_mix_post", "n_x_post", "n_f_post"]):
            S.dma("sp", wpost[:, k, :], vecs[n].to_broadcast([128, D]), (), [B_const])
        S.dma("sp", glaw[:], vecs["gla_nw"].to_broadcast([128, 128]), (), [B_const])
        S.dma("sp", subw[:], vecs["sub_w"].to_broadcast([128, 128]), (), [B_const])
        S.dma("sp", wup[0:16, 0:256], wgu_f[:, :], [B_const], [B_const])
        S.dma("sp", wup[16:32, 256:512], wgu_b[:, :], [B_const], [B_const])
        S.dma("sp", wup[32:33, 0:256], vecs["bg_f"], [B_const], [B_const])
        S.dma("sp", wup[32:33, 256:512], vecs["bg_b"], [B_const], [B_const])
        S.dma("sp", mkt[:].rearrange("p a b -> p (a b)"), mk_d[:, :], (), [B_const])
        S.dma("sp", posb[:], posb_d[:, :], (), [B_const])
        S.op("dve", lambda e: e.tensor_scalar(out=subw[:], in0=subw[:], scalar1=1.0 - LAM_INIT, scalar2=None, op0=ALU.mult),
             [B_const], [B_const])
        S.op("dve", lambda e: e.tensor_scalar(out=omk[:], in0=mkt[:], scalar1=-1.0, scalar2=1.0, op0=ALU.mult, op1=ALU.add),
             [B_const], [B_const])
        lt = sm[0:1, 0:4]
        lv = K.arena[0:1, 0:1024].bitcast(F32)
        for k, n in enumerate(["lq1", "lk1", "lq2", "lk2"]):
            S.dma("sp", lv[:, k * 64:(k + 1) * 64], vecs[n], [B_const], [B_const])
        S.op("dve", lambda e: e.tensor_tensor(out=lv[:, 256:320], in0=lv[:, 0:64], in1=lv[:, 64:128], op=ALU.mult), [B_const], [B_const])
        S.op("dve", lambda e: e.tensor_tensor(out=lv[:, 320:384], in0=lv[:, 128:192], in1=lv[:, 192:256], op=ALU.mult), [B_const], [B_const])
        S.op("dve", lambda e: e.tensor_reduce(out=lt[:, 0:2], in_=lv[:, 256:384].rearrange("p (a b) -> p a b", a=2), axis=AX.X, op=ALU.add),
             [B_const], [B_const])
        S.op("act", lambda e: e.activation(out=lt[:, 0:2], in_=lt[:, 0:2], func=AF.Exp), [B_const], [B_const])
        S.op("dve", lambda e: e.tensor_tensor(out=lt[:, 2:3], in0=lt[:, 1:2], in1=lt[:, 0:1], op=ALU.subtract), [B_const], [B_const])
        S.op("dve", lambda e: e.tensor_scalar(out=lt[:, 2:3], in0=lt[:, 2:3], scalar1=-LAM_INIT, scalar2=None, op0=ALU.add), [B_const], [B_const])
        pf, pbf, pbuf = K.psum()
        S.op("pe", lambda e: e.matmul(pf[:, 0:2], lhsT=onesf[0:1, :], rhs=lt[:, 2:4], start=True, stop=True), [B_const], [pbuf])
        S.op("dve", lambda e: e.tensor_copy(out=smallc[:, 0:1], in_=pf[:, 0:1]), [pbuf, B_const], [B_const])

    inv_freq = [THETA ** (-(2.0 * i) / 16.0) for i in range(8)]

    def make_tables(ntile, tile0_is_posb, cs_out, sc_out, scale, work):
        posi = work[:, 0, :, 0:1].rearrange("p a b -> p (a b)")
        ang = work[:, 1, :, :]
        kk = work[:, 2, :, :]
        tt = work[:, 3, :, :]
        pi_ = K.sb("posi%d" % make_tables.n, [128, ntile], I32)
        pf_ = K.sb("posf%d" % make_tables.n, [128, ntile], F32)
        make_tables.n += 1
        S.op("pool", lambda e: e.iota(pi_[:], pattern=[[128, ntile]], base=0, channel_multiplier=1), (), [B_const])
        S.op("dve", lambda e: e.tensor_copy(out=pf_[:], in_=pi_[:]), [B_const], [B_const])
        if tile0_is_posb:
            S.op("dve", lambda e: e.tensor_scalar(out=pf_[:], in0=pf_[:], scalar1=posb[:, 0:1], scalar2=None, op0=ALU.add), [B_const], [B_const])
        for i in range(8):
            S.op("dve", lambda e, i=i: e.tensor_scalar(out=ang[:, :, i], in0=pf_[:], scalar1=float(np.float32(inv_freq[i])), scalar2=None,
                                                       op0=ALU.mult), [B_const], [B_const])
        MAGIC = 12582912.0
        c1 = float(np.float32(2 * math.pi))
        c2 = float(2 * math.pi - c1)

        def reduce_sin(dst, shift):
            if shift != 0.0:
                S.op("dve", lambda e: e.tensor_scalar(out=tt, in0=ang, scalar1=shift, scalar2=None, op0=ALU.add), [B_const], [B_const])
                src = tt
            else:
                src = ang
            S.op("dve", lambda e: e.tensor_scalar(out=kk, in0=src, scalar1=1.0 / (2 * math.pi), scalar2=MAGIC, op0=ALU.mult, op1=ALU.add), [B_const], [B_const])
            S.op("dve", lambda e: e.tensor_scalar(out=kk, in0=kk, scalar1=-MAGIC, scalar2=None, op0=ALU.add), [B_const], [B_const])
            S.op("dve", lambda e: e.scalar_tensor_tensor(out=tt, in0=kk, scalar=-c1, in1=src, op0=ALU.mult, op1=ALU.add), [B_const], [B_const])
            S.op("dve", lambda e: e.scalar_tensor_tensor(out=tt, in0=kk, scalar=-c2, in1=tt, op0=ALU.mult, op1=ALU.add), [B_const], [B_const])
            S.op("dve", lambda e: e.tensor_scalar(out=kk, in0=tt, scalar1=math.pi, scalar2=-2 * math.pi, op0=ALU.is_gt, op1=ALU.mult), [B_const], [B_const])
            S.op("dve", lambda e: e.tensor_tensor(out=tt, in0=tt, in1=kk, op=ALU.add), [B_const], [B_const])
            S.op("dve", lambda e: e.tensor_scalar(out=kk, in0=tt, scalar1=-math.pi, scalar2=2 * math.pi, op0=ALU.is_lt, op1=ALU.mult), [B_const], [B_const])
            S.op("dve", lambda e: e.tensor_tensor(out=tt, in0=tt, in1=kk, op=ALU.add), [B_const], [B_const])
            S.op("act", lambda e: e.activation(out=tt, in_=tt, func=AF.Sin), [B_const], [B_const])
            for d in dst:
                S.op("dve", lambda e, d=d: e.tensor_scalar(out=d, in0=tt, scalar1=scale, scalar2=None, op0=ALU.mult), [B_const], [B_const])

        reduce_sin([cs_out[:, :, 8:16], sc_out[:, :, 0:8]], 0.0)
        reduce_sin([cs_out[:, :, 0:8], sc_out[:, :, 8:16]], math.pi / 2)
    make_tables.n = 0

    def rstd_from(ss_ap, n_el, out_ap, rb, wb, ncols=1):
        S.op("dve", lambda e: e.tensor_scalar(out=out_ap, in0=ss_ap, scalar1=1.0 / n_el, scalar2=EPS, op0=ALU.mult, op1=ALU.add), rb, wb)
        S.op("pool", lambda e: e.tensor_tensor(out=out_ap, in0=out_ap, in1=smallc[:, 2:2 + ncols], op=ALU.pow), wb + [B_const], wb)

    def load_norm_T(src_ap, dst_hT, dst_buf, keep_x=None):
        if keep_x is None:
            xa, xb_ = xslot()
            S.dma("sp", xa, src_ap, (), [xb_])
        else:
            xa, xb_ = keep_x
        norm_T(xa, xb_, dst_hT, dst_buf)

    def norm_x(xa, xb_, xn=None, xnb_=None):
        st_, sb_ = smslot()
        S.op("act", lambda e: e.activation(out=junk[:], in_=xa, func=AF.Square, accum_out=st_[:, 0:1]), [xb_], [b_junk, sb_])
        rstd_from(st_[:, 0:1], D, st_[:, 1:2], [sb_], [sb_])
        if xn is None:
            xn, xnb_ = xnslot()
        S.op("act", lambda e: e.activation(out=xn, in_=xa, func=AF.Copy, scale=st_[:, 1:2]), [xb_, sb_], [xnb_])
        return xn, xnb_

    def xT_from(xn, xnb_, dst_hT, dst_buf):
        pf, pbf, pbuf = K.psum()

        def tr(e):
            ins = None
            for c in range(KC):
                ins = e.transpose(out=pbf[:, c * 128:(c + 1) * 128], in_=xn[:, c * 128:(c + 1) * 128], identity=identb[:])
            return ins
        S.op("pe", tr, [xnb_, B_const], [pbuf])
        copy_op(K, K.alt(), dst_hT, pbf.rearrange("p (a b) -> p a b", a=KC), [pbuf], [dst_buf])

    def norm_T(xa, xb_, dst_hT, dst_buf):
        xn, xnb_ = norm_x(xa, xb_)
        xT_from(xn, xnb_, dst_hT, dst_buf)

    class XRing:
        def __init__(self, n):
            self.n = n
            self.x = [K.ar([128, D], F32) for _ in range(n)]
            self.xn = [K.ar([128, D], BF16) for _ in range(n)]
            self.bx = bufs(n)
            self.bxn = bufs(n)

        def stage0(self, idx, src_ap):
            j = idx % self.n
            S.dma("sp", self.x[j], src_ap, (), [self.bx[j]])
            norm_x(self.x[j], self.bx[j], self.xn[j], self.bxn[j])

        def get(self, idx):
            j = idx % self.n
            return self.xn[j], self.bxn[j]

    def proj(dst_ps, hT_tile, w_ap, c0, n):
        def f(e):
            ins = None
            for c in range(KC):
                ins = e.matmul(dst_ps[:, 0:n], lhsT=hT_tile[:, c, :], rhs=w_ap[:, c, c0:c0 + n], start=(c == 0), stop=(c == KC - 1))
            return ins
        return f

    def rope(pp, pbuf, dst, dbuf, cs, sc, tmp, tbuf, rest_scale):
        v = pp.rearrange("p (h d) -> p h d", h=8)
        dv_ = dst.rearrange("p (h d) -> p h d", h=8)
        csb = cs.unsqueeze(1).to_broadcast([128, 8, 16])
        scb = sc.unsqueeze(1).to_broadcast([128, 8, 16])
        t = tmp[:, 0:128].rearrange("p (h d) -> p h d", h=8)
        u = tmp[:, 128:256].rearrange("p (h d) -> p h d", h=8)
        S.op("dve", lambda e: e.tensor_tensor(out=t, in0=v[:, :, 0:16], in1=csb, op=ALU.mult), [pbuf, B_const], [tbuf])
        S.op("dve", lambda e: e.tensor_tensor(out=u, in0=v[:, :, 0:16], in1=scb, op=ALU.mult), [pbuf, B_const], [tbuf])
        S.op("dve", lambda e: e.tensor_tensor(out=dv_[:, :, 0:8], in0=t[:, :, 0:8], in1=t[:, :, 8:16], op=ALU.subtract), [tbuf], [dbuf])
        S.op("dve", lambda e: e.tensor_tensor(out=dv_[:, :, 8:16], in0=u[:, :, 0:8], in1=u[:, :, 8:16], op=ALU.add), [tbuf], [dbuf])
        if rest_scale is None:
            S.op("act", lambda e: e.activation(out=dv_[:, :, 16:64], in_=v[:, :, 16:64], func=AF.Copy), [pbuf], [dbuf])
        else:
            S.op("act", lambda e: e.activation(out=dv_[:, :, 16:64], in_=v[:, :, 16:64], func=AF.Copy, scale=rest_scale), [pbuf], [dbuf])

    def transpose4(src, sbuf_, dst_ap, dbuf):
        pf, pbf, pbuf = K.psum()

        def tr(e):
            ins = None
            for c in range(4):
                ins = e.transpose(out=pbf[:, c * 128:(c + 1) * 128], in_=src[:, c * 128:(c + 1) * 128], identity=identb[:])
            return ins
        S.op("pe", tr, [sbuf_, B_const], [pbuf])
        copy_op(K, K.alt(), dst_ap, pbf[:, 0:512].rearrange("p (a b) -> p a b", a=4), [pbuf], [dbuf])

    def glT_proj(hT_tile, hbuf, w_ap, wbuf, c0, glT_ap, glT_buf):
        pf, _, pbuf = K.psum()

        def f(e):
            ins = None
            for c in range(KC):
                ins = e.matmul(pf[:, 0:128], lhsT=w_ap[:, c, c0:c0 + 128], rhs=hT_tile[:, c, :], start=(c == 0), stop=(c == KC - 1))
            return ins
        S.op("pe", f, [hbuf, wbuf], [pbuf])
        S.op("dve", lambda e: e.tensor_copy(out=glT_ap[0:32, :], in_=pf[0:32, 0:128]), [pbuf], [glT_buf])

    def gates_from_glT(glT_ap, glT_buf, A):
        pf2, _, pbuf2 = K.psum()
        S.op("pe", lambda e: e.matmul(pf2[:, 0:512], lhsT=glT_ap[0:33, :], rhs=wup[0:33, :], start=True, stop=True),
             [glT_buf, B_const], [pbuf2])
        S.op("act", lambda e: e.activation(out=A["gp"], in_=pf2[:, 0:512], func=AF.Exp, scale=-1.0), [pbuf2], [A["b_gp"]])
        S.op("act", lambda e: e.activation(out=A["gp"], in_=A["gp"], func=AF.Ln, bias=1.0), [A["b_gp"]], [A["b_gp"]])

    def token_cumsum_kd(A, gk, pb_qk, kd_out, kd_buf, masks=None):
        pf, _, pbuf = K.psum()

        def f(e):
            e.matmul(pf[:, 0:256], lhsT=Tgt[:], rhs=A["gp"][:, 0:256], start=True, stop=True)
            return e.matmul(pf[:, 256:512], lhsT=Tlt[:], rhs=A["gp"][:, 256:512], start=True, stop=True)
        S.op("pe", f, [A["b_gp"], B_const], [pbuf])
        S.op("act", lambda e: e.activation(out=A["ec"], in_=pf[:, 0:512], func=AF.Exp), [pbuf], [A["b_ec"]])
        if masks is None:
            S.op("dve", lambda e: e.tensor_tensor(out=kd_out, in0=A["ec"].rearrange("p (a b) -> p a b", a=2),
                                                  in1=gk.unsqueeze(1).to_broadcast([128, 2, 256]), op=ALU.mult), [A["b_ec"], pb_qk], [kd_buf])
        else:
            for d_ in range(2):
                S.op("dve", lambda e, d_=d_: e.scalar_tensor_tensor(out=kd_out[:, d_, :], in0=A["ec"][:, d_ * 256:(d_ + 1) * 256], scalar=masks[d_],
                                                                    in1=gk, op0=ALU.mult, op1=ALU.mult), [A["b_ec"], pb_qk, B_const], [kd_buf])

    def total_decay(A, dec_out, dec_buf):
        pf, _, pbuf = K.psum()

        def f(e):
            ins = None
            for j in range(4):
                ins = e.matmul(pf[:, j:j + 1], lhsT=A["gp"][:, j * 128:(j + 1) * 128], rhs=smallc[:, 1:2], start=True, stop=True)
            return ins
        S.op("pe", f, [A["b_gp"], B_const], [pbuf])
        S.op("act", lambda e: e.activation(out=dec_out, in_=pf[:, 0:4], func=AF.Exp), [pbuf], [dec_buf])

    def u_matmuls(kd, kd_buf, vg, vg_buf, dirs):
        res = []
        for d_ in dirs:
            pf, _, pbuf = K.psum()

            def f(e, d_=d_, pf=pf):
                ins = None
                for h in range(4):
                    pr = h // 2
                    ins = e.matmul(pf[:, h * 128:(h + 1) * 128], lhsT=kd[:, d_, pr * 128:(pr + 1) * 128], rhs=vg[:, h * 128:(h + 1) * 128],
                                   start=True, stop=True)
                return ins
            S.op("pe", f, [kd_buf, vg_buf], [pbuf])
            res.append((pf, pbuf))
        return res

    b_W = {n: Buf() for n in ["WIN", "WOUT", "WXQ", "WXKV", "WXO", "WGU", "WD"]}

    def weight_prep():
        K.arena_reset()
        stg = [K.ar([128, KC, 512], F32) for _ in range(2)]
        cvt = [K.ar([128, KC, 512], BF16) for _ in range(2)]
        b_stg = bufs(2)
        b_cvt = bufs(2)
        cnt = [0]

        def block(src_w, c0, n, gain_k, stores):
            i = cnt[0] % 2
            cnt[0] += 1
            S.dma("sp", stg[i][:, :, 0:n], src_w.rearrange("(c p) n -> p c n", p=128)[:, :, c0:c0 + n], (), [b_stg[i]])
            if gain_k is None:
                S.op("dve", lambda e: e.tensor_copy(out=cvt[i][:, :, 0:n], in_=stg[i][:, :, 0:n]), [b_stg[i]], [b_cvt[i]])
            else:
                for c in range(KC):
                    S.op("act", lambda e, c=c: e.activation(out=cvt[i][:, c, 0:n], in_=stg[i][:, c, 0:n], func=AF.Copy, scale=wpre[:, gain_k, c:c + 1]),
                         [b_stg[i], B_const], [b_cvt[i]])
            for dst, off, m, wb in stores:
                S.dma("pool", dst, cvt[i][:, :, off:off + m], [b_cvt[i]], [wb])

        for c0 in range(0, INC, 512):
            n = min(512, INC - c0)
            block(w_in, c0, n, 0, [(WIN_d[:, :, c0:c0 + n], 0, n, b_W["WIN"])])
        for c0 in range(0, D, 512):
            block(w_out, c0, 512, None, [(WOUT_d[:, :, c0:c0 + 512], 0, 512, b_W["WOUT"])])
            block(w_xq, c0, 512, 1, [(WXQ_d[:, :, c0:c0 + 512], 0, 512, b_W["WXQ"])])
            block(w_xo, c0, 512, None, [(WXO_d[:, :, c0:c0 + 512], 0, 512, b_W["WXO"])])
        for c0 in range(0, 2 * D, 512):
            block(w_xkv, c0, 512, 3, [(WXKV_d[:, :, c0:c0 + 512], 0, 512, b_W["WXKV"])])
        for gi, wsrc in enumerate((w_fg, w_fu)):
            for c0 in range(0, DFF, 512):
                n = min(512, DFF - c0)
                stores = []
                for j in range(n // 128):
                    fc = c0 // 128 + j
                    stores.append((WGU_d[fc, :, gi, :, :], j * 128, 128, b_W["WGU"]))
                block(wsrc, c0, n, 2, stores)
        for f0 in range(0, FC, 4):
            nf = min(4, FC - f0)
            i = cnt[0] % 2
            cnt[0] += 1
            sv = stg[i].rearrange("p a b -> p (a b)")[:, 0:nf * D].rearrange("p (a b) -> p a b", a=nf)
            cv = cvt[i].rearrange("p a b -> p (a b)")[:, 0:nf * D].rearrange("p (a b) -> p a b", a=nf)
            S.dma("sp", sv, w_fd[f0 * 128:(f0 + nf) * 128, :].rearrange("(a p) n -> p a n", p=128), (), [b_stg[i]])
            S.op("dve", lambda e, cv=cv, sv=sv: e.tensor_copy(out=cv, in_=sv), [b_stg[i]], [b_cvt[i]])
            S.dma("pool", WD_d[f0:f0 + nf].rearrange("a p n -> p a n"), cv, [b_cvt[i]], [b_W["WD"]])

    def gla_work():
        A = {}
        A["gk"] = [K.ar([128, 256], F32) for _ in range(2)]
        A["b_gks"] = bufs(2)
        A["glT"] = [K.ar([128, 128], F32) for _ in range(2)]
        A["b_glTs"] = bufs(2)
        A["gp"] = K.ar([128, 512], F32)
        A["ec"] = K.ar([128, 512], F32)
        for n in ("gp", "ec"):
            A["b_" + n] = Buf()
        for j in range(2):
            S.op("pool", lambda e, j=j: e.memset(A["glT"][j][32:64, :], 1.0), (), [A["b_glTs"][j]])
        return A

    Sst = K.sb("Sst", [128, 2, 2, 128], F32)
    Pst = K.sb("Pst", [128, 2], F32)
    b_Sst = Buf()

    def pre_phase():
        K.arena_reset()
        PC = 512 + 512 + 256 + 512 + 128
        WP = K.ar([128, KC, PC], BF16)
        b_WP = Buf()
        for dst0, src0, n in ((0, 2080, 512), (512, 2592, 512), (1024, 256, 256), (1280, 512, 512), (1792, 1024, 128)):
            S.dma("sp", WP[:, :, dst0:dst0 + n], WIN_d[:, :, src0:src0 + n], [b_W["WIN"]], [b_WP])
        KTs = K.ar([128, 4, UT], BF16)
        VPs = K.ar([128, 4, NT, VW], BF16)
        b_KTs, b_VPs = Buf(), Buf()
        S.op("pool", lambda e: e.memset(VPs[:], 1.0), (), [b_VPs])
        ktab = K.ar([128, 2, T, 16], F32)
        mark_ = K.arena_off
        work = K.ar([128, 4, T, 8], F32)
        make_tables(T, False, ktab[:, 0], ktab[:, 1], 1.0, work)
        S.barrier()
        K.arena_off = mark_
        A = gla_work()
        hTt = [K.ar([128, KC, 128], BF16) for _ in range(2)]
        b_hTt = bufs(2)
        krot = [K.ar([128, 512], BF16) for _ in range(3)]
        b_krot = bufs(3)
        rtmp = K.ar([128, 256], F32)
        b_rtmp = Buf()
        kd = [K.ar([128, 2, 256], BF16) for _ in range(2)]
        b_kd = bufs(2)
        vg = [K.ar([128, 512], BF16) for _ in range(3)]
        b_vg = bufs(3)
        dec = [K.ar([128, 8], F32) for _ in range(2)]
        b_dec = bufs(2)
        S.op("pool", lambda e: e.memset(Sst[:], 0.0), (), [b_Sst])
        S.op("pool", lambda e: e.memset(Pst[:], 1.0), [b_Sst], [b_Sst])

        xr = XRing(2)

        def st1(g):
            seg, lt_ = divmod(g, NT)
            i2, i3 = g % 2, g % 3
            xT_from(*xr.get(g), hTt[i2], b_hTt[i2])
            glT_proj(hTt[i2], b_hTt[i2], WP, b_WP, 1792, A["glT"][i2], A["b_glTs"][i2])
            pg, _, pgb = K.psum()
            S.op("pe", proj(pg, hTt[i2], WP, 1024, 256), [b_hTt[i2], b_WP], [pgb])
            copy_op(K, "act", A["gk"][i2], pg[:, 0:256], [pgb], [A["b_gks"][i2]])
            pgv, _, pgvb = K.psum()
            S.op("pe", proj(pgv, hTt[i2], WP, 1280, 512), [b_hTt[i2], b_WP], [pgvb])
            copy_op(K, "act", vg[i3], pgv[:, 0:512], [pgvb], [b_vg[i3]])
            pk, _, pkb = K.psum()
            S.op("pe", proj(pk, hTt[i2], WP, 0, 512), [b_hTt[i2], b_WP], [pkb])
            rope(pk, pkb, krot[i3], b_krot[i3], ktab[:, 0, g, :], ktab[:, 1, g, :], rtmp, b_rtmp, None)
            pv, _, pvb = K.psum()
            S.op("pe", proj(pv, hTt[i2], WP, 512, 512), [b_hTt[i2], b_WP], [pvb])
            copy_op(K, "dve", VPs[:, :, lt_, 0:128], pv.rearrange("p (h d) -> p h d", h=4), [pvb], [b_VPs])
            if lt_ == NT - 1:
                for h in range(4):
                    S.dma("pool", VP_d[seg, h], VPs[:, h].rearrange("p a b -> p (a b)"), [b_VPs], [b_KV[seg]])

        def st2(g):
            i2 = g % 2
            gates_from_glT(A["glT"][i2], A["b_glTs"][i2], A)
            token_cumsum_kd(A, A["gk"][i2], A["b_gks"][i2], kd[i2], b_kd[i2], masks=(mkt[:, 0, g:g + 1], mkt[:, 1, g:g + 1]))
            dc = dec[i2]
            total_decay(A, dc[:, 0:4], b_dec[i2])
            d4 = dc[:, 0:4].rearrange("p (a b) -> p a b", a=2)
            e4 = dc[:, 4:8].rearrange("p (a b) -> p a b", a=2)
            S.op("dve", lambda e, g=g, d4=d4, e4=e4: e.tensor_tensor(out=e4, in0=d4, in1=mkt[:, :, g:g + 1].to_broadcast([128, 2, 2]), op=ALU.mult),
                 [b_dec[i2], B_const], [b_dec[i2]])
            S.op("dve", lambda e, g=g, e4=e4: e.tensor_tensor(out=e4, in0=e4, in1=omk[:, :, g:g + 1].to_broadcast([128, 2, 2]), op=ALU.add),
                 [b_dec[i2], B_const], [b_dec[i2]])

        def st3(g):
            seg, lt_ = divmod(g, NT)
            i2, i3 = g % 2, g % 3
            dc = dec[i2]
            transpose4(krot[i3], b_krot[i3], KTs[:, :, lt_ * 128:(lt_ + 1) * 128], b_KTs)
            (puf, pufb), (pub, pubb) = u_matmuls(kd[i2], b_kd[i2], vg[i3], b_vg[i3], (0, 1))
            for h in range(4):
                pr, lo = h // 2, (h % 2) * 64
                S.op("dve", lambda e, h=h, pr=pr, lo=lo, puf=puf, dc=dc: e.scalar_tensor_tensor(
                    out=Sst[lo:lo + 64, 0, pr, :], in0=Sst[lo:lo + 64, 0, pr, :], scalar=dc[lo:lo + 64, 4 + pr:5 + pr],
                    in1=puf[lo:lo + 64, h * 128:(h + 1) * 128], op0=ALU.mult, op1=ALU.add), [pufb, b_dec[i2], b_Sst], [b_Sst])
                S.op("dve", lambda e, h=h, pr=pr, lo=lo, pub=pub: e.scalar_tensor_tensor(
                    out=Sst[lo:lo + 64, 1, pr, :], in0=pub[lo:lo + 64, h * 128:(h + 1) * 128], scalar=Pst[lo:lo + 64, pr:pr + 1],
                    in1=Sst[lo:lo + 64, 1, pr, :], op0=ALU.mult, op1=ALU.add), [pubb, b_Sst], [b_Sst])
            S.op("dve", lambda e, dc=dc: e.tensor_tensor(out=Pst[:], in0=Pst[:], in1=dc[:, 6:8], op=ALU.mult), [b_Sst, b_dec[i2]], [b_Sst])
            if lt_ == NT - 1:
                for h in range(4):
                    S.dma("pool", KT_d[seg, h], KTs[:, h, :], [b_KTs], [b_KV[seg]])
        for g in range(-3, T):
            if 0 <= g + 3 < T:
                xr.stage0(g + 3, xpf[(g + 3) * 128:(g + 4) * 128, :])
            if 0 <= g + 2 < T:
                st1(g + 2)
            if 0 <= g + 1 < T:
                st2(g + 1)
            if g >= 0:
                st3(g)
        if debug:
            S.dma("pool", SDBG_d.rearrange("p a b c -> p (a b c)"), Sst[:].rearrange("p a b c -> p (a b c)"), [b_Sst], [Buf()])

    b_KV = bufs(cfg.NC)
    b_MIX = [[Buf() for _ in range(2 * (UT // HB))] for _ in range(NU)]

    def phase_G(u, xsrc):
        K.arena_reset()
        QE = K.ar([128, 2, 2, UT], BF16)
        KE = K.ar([128, 2, 2, UT], BF16)
        VG = K.ar([128, NT, 512], BF16)
        KDB = K.ar([128, NT, 256], BF16)
        SFP = K.ar([128, NT, 2, 128], BF16)
        DECB = K.ar([128, NT, 2], F32)
        b_QE, b_KE, b_VG, b_KDB, b_SFP, b_DECB = bufs(NT), bufs(NT), bufs(NT), bufs(NT), bufs(NT), bufs(NT)
        SF = K.ar([128, 2, 128], F32)
        SB_ = K.ar([128, 2, 128], F32)
        SBb = K.ar([128, 2, 128], BF16)
        b_SF, b_SB, b_SBb = Buf(), Buf(), Buf()
        mark = K.arena_off
        GC = 1056
        WG_ = K.ar([128, KC, GC], BF16)
        b_WG = Buf()
        for c0 in range(0, GC, 512):
            n = min(512, GC - c0)
            S.dma("sp", WG_[:, :, c0:c0 + n], WIN_d[:, :, c0:c0 + n], [b_W["WIN"]], [b_WG])
        A = gla_work()
        glow_sb = [K.ar([128, 32], F32) for _ in range(2)]
        b_glow_sb = bufs(2)
        qk = K.ar([128, 512], BF16)
        b_qk = Buf()
        eqk = K.ar([128, 2, 4, 128], F32)
        b_eqk = Buf()
        kdf = K.ar([128, 2, 256], BF16)
        b_kdf = Buf()
        dec = K.ar([128, 4], F32)
        b_dec = Buf()
        lnb = K.ar([128, 2], F32)
        S.op("pool", lambda e: e.memset(lnb, math.log(0.125)), (), [b_eqk])
        if u == 0:
            S.op("dve", lambda e: e.tensor_copy(out=SF, in_=Sst[:, 0]), [b_Sst], [b_SF])
            S.op("dve", lambda e: e.tensor_copy(out=SB_, in_=Sst[:, 1]), [b_Sst], [b_SB])
        else:
            S.op("pool", lambda e: e.memset(SF, 0.0), (), [b_SF])
            S.op("pool", lambda e: e.memset(SB_, 0.0), (), [b_SB])
        qks = [qk, K.ar([128, 512], BF16)]
        b_qks = [b_qk, Buf()]

        xr = XRing(2)

        def gfront(i):
            tok = slice(i * 128, (i + 1) * 128)
            i2 = i % 2
            xT_from(*xr.get(i), hT[:, :, tok], b_hT[i])
            hTi = hT[:, :, tok]
            pq, _, pqb = K.psum()
            S.op("pe", proj(pq, hTi, WG_, 0, 512), [b_hT[i], b_WG], [pqb])
            pgv, _, pgvb = K.psum()
            S.op("pe", proj(pgv, hTi, WG_, 512, 512), [b_hT[i], b_WG], [pgvb])
            pgl, _, pglb = K.psum()
            S.op("pe", proj(pgl, hTi, WG_, 1024, 32), [b_hT[i], b_WG], [pglb])
            copy_op(K, "act", VG[:, i, :], pgv[:, 0:512], [pgvb], [b_VG[i]])
            copy_op(K, "dve", qks[i2], pq[:, 0:512], [pqb], [b_qks[i2]])
            copy_op(K, "dve", A["gk"][i2], pq[:, 256:512], [pqb], [A["b_gks"][i2]])
            copy_op(K, "dve", glow_sb[i2], pgl[:, 0:32], [pglb], [b_glow_sb[i2]])

        def gback(i):
            tok = slice(i * 128, (i + 1) * 128)
            i2 = i % 2
            qk_ = qks[i2]
            pfg, _, pbg = K.psum()
            S.op("pe", lambda e, pfg=pfg, i2=i2: e.transpose(out=pfg[0:32, 0:128], in_=glow_sb[i2], identity=identf[:]), [b_glow_sb[i2], B_const], [pbg])
            S.op("dve", lambda e, pfg=pfg: e.tensor_copy(out=A["glT"][0][0:32, :], in_=pfg[0:32, 0:128]), [pbg], [A["b_glTs"][0]])
            gates_from_glT(A["glT"][0], A["b_glTs"][0], A)
            token_cumsum_kd(A, A["gk"][i2], A["b_gks"][i2], kdf, b_kdf)
            S.op("pool", lambda e, i=i: e.tensor_copy(out=KDB[:, i, :], in_=kdf[:, 1, :]), [b_kdf], [b_KDB[i]])
            total_decay(A, dec, b_dec)
            S.op("dve", lambda e, i=i: e.tensor_copy(out=DECB[:, i, :], in_=dec[:, 2:4]), [b_dec], [b_DECB[i]])
            pc, _, pcb = K.psum()

            def fcs(e, pc=pc):
                ins = None
                for j in range(4):
                    ins = e.matmul(pc[:, j * 128:(j + 1) * 128], lhsT=A["gp"][:, j * 128:(j + 1) * 128], rhs=(Tle if j < 2 else Tge)[:],
                                   start=True, stop=True)
                return ins
            S.op("pe", fcs, [A["b_gp"], B_const], [pcb])
            S.op("act", lambda e, pc=pc: e.activation(out=eqk[:, 0].rearrange("p a b -> p (a b)"), in_=pc[:, 0:512], func=AF.Exp, bias=lnb[:, 0:1]), [pcb, b_eqk], [b_eqk])
            S.op("act", lambda e, pc=pc: e.activation(out=eqk[:, 1].rearrange("p a b -> p (a b)"), in_=pc[:, 0:512], func=AF.Exp, scale=-1.0), [pcb, b_eqk], [b_eqk])
            pt, ptb, ptbuf = K.psum()

            def trq(e, ptb=ptb, qk_=qk_):
                ins = None
                for c in range(4):
                    ins = e.transpose(out=ptb[:, c * 128:(c + 1) * 128], in_=qk_[:, c * 128:(c + 1) * 128], identity=identb[:])
                return ins
            S.op("pe", trq, [b_qks[i2], B_const], [ptbuf])
            qT = ptb[:, 0:256].rearrange("p (a b) -> p a b", a=2)
            kT = ptb[:, 256:512].rearrange("p (a b) -> p a b", a=2)
            e_q = eqk[:, 0].rearrange("p (d a) b -> p d a b", d=2)
            e_k = eqk[:, 1].rearrange("p (d a) b -> p d a b", d=2)
            for d_ in range(2):
                S.op("dve", lambda e, d_=d_, qT=qT, e_q=e_q, tok=tok: e.tensor_tensor(out=QE[:, d_, :, tok], in0=qT, in1=e_q[:, d_], op=ALU.mult),
                     [ptbuf, b_eqk], [b_QE[i]])
                S.op("dve", lambda e, d_=d_, kT=kT, e_k=e_k, tok=tok: e.tensor_tensor(out=KE[:, d_, :, tok], in0=kT, in1=e_k[:, d_], op=ALU.mult),
                     [ptbuf, b_eqk], [b_KE[i]])
            S.op("dve", lambda e, i=i: e.tensor_copy(out=SFP[:, i], in_=SF), [b_SF], [b_SFP[i]])
            ((puf, pufb),) = u_matmuls(kdf, b_kdf, VG[:, i, :], b_VG[i], (0,))
            for h in range(4):
                pr, lo = h // 2, (h % 2) * 64
                S.op("dve", lambda e, h=h, pr=pr, lo=lo, puf=puf: e.scalar_tensor_tensor(
                    out=SF[lo:lo + 64, pr, :], in0=SF[lo:lo + 64, pr, :], scalar=dec[lo:lo + 64, pr:pr + 1],
                    in1=puf[lo:lo + 64, h * 128:(h + 1) * 128], op0=ALU.mult, op1=ALU.add), [pufb, b_dec, b_SF], [b_SF])
        for i in range(-2, NT):
            if 0 <= i + 2 < NT:
                xr.stage0(i + 2, xsrc[(i + 2) * 128:(i + 3) * 128, :])
            if 0 <= i + 1 < NT:
                gfront(i + 1)
            if i >= 0:
                gback(i)
        if STOP == "G%da" % u:
            return
        S.barrier()
        K.arena_off = mark
        Wog = K.ar([128, KC, 512], BF16)
        b_Wog = Buf()
        S.dma("sp", Wog, WIN_d[:, :, 1056:1568], [b_W["WIN"]], [b_Wog])
        AM = [K.ar([128, 2, 4, 128], BF16) for _ in range(2)]
        b_AM = bufs(2)
        o32 = K.ar([128, 512], F32)
        b_o32 = Buf()
        sq = K.ar([128, 512], F32)
        b_sq = Buf()
        sg = K.ar([128, 512], F32)
        b_sg = Buf()
        ogs = [K.ar([128, 512], BF16) for _ in range(2)]
        b_ogs = bufs(2)
        ogl = [K.ar([128, 512], BF16) for _ in range(2)]
        b_ogl = bufs(2)
        stg = [K.ar([128, 4, QB], BF16) for _ in range(2)]
        b_stg = bufs(2)
        kdb2 = [K.ar([128, 2, 256], BF16) for _ in range(2)]
        b_kdb2 = bufs(2)
        QEm = [K.ar([128, 2, 4, 128], BF16) for _ in range(2)]
        b_QEm = bufs(2)

        def p2front(i):
            tok = slice(i * 128, (i + 1) * 128)
            i2 = i % 2
            pog, _, pogb = K.psum()
            S.op("pe", proj(pog, hT[:, :, tok], Wog, 0, 512), [b_hT[i], b_Wog], [pogb])
            S.op("act", lambda e, pog=pog: e.activation(out=sg, in_=pog[:, 0:512], func=AF.Exp, scale=-1.0), [pogb], [b_sg])
            S.op("act", lambda e: e.activation(out=sg, in_=sg, func=AF.Ln, bias=1.0), [b_sg], [b_sg])
            S.op("act", lambda e: e.activation(out=sg, in_=sg, func=AF.Exp, scale=-1.0), [b_sg], [b_sg])
            S.op("dve", lambda e, pog=pog, i2=i2: e.tensor_tensor(out=ogs[i2], in0=pog[:, 0:512], in1=sg, op=ALU.mult), [pogb, b_sg], [b_ogs[i2]])
            for d_ in range(2):
                S.op("pool", lambda e, d_=d_, i2=i2, tok=tok: e.tensor_tensor(
                    out=QEm[i2][:, d_].rearrange("p (a b) t -> p a b t", a=2), in0=QE[:, d_, :, tok].unsqueeze(2).to_broadcast([128, 2, 2, 128]),
                    in1=hm[:].unsqueeze(1).unsqueeze(3).to_broadcast([128, 2, 2, 128]), op=ALU.mult), [b_QE[i], B_const], [b_QEm[i2]])
            pa = [K.psum() for _ in range(2)]
            for d_ in range(2):
                def fa(e, d_=d_, pf=pa[d_][0], tok=tok, i2=i2):
                    ins = None
                    for h in range(4):
                        pr = h // 2
                        ins = e.matmul(pf[:, h * 128:(h + 1) * 128], lhsT=KE[:, d_, pr, tok], rhs=QEm[i2][:, d_, h, :], start=True, stop=True)
                    return ins
                S.op("pe", fa, [b_KE[i], b_QEm[i2]], [pa[d_][2]])
                mask = Mle if d_ == 0 else Mge
                S.op("dve", lambda e, d_=d_, mask=mask, pf=pa[d_][0], i2=i2: e.tensor_tensor(
                    out=AM[i2][:, d_], in0=pf.rearrange("p (h t) -> p h t", h=4), in1=mask[:].unsqueeze(1).to_broadcast([128, 4, 128]), op=ALU.mult),
                    [pa[d_][2], B_const], [b_AM[i2]])
            if i > 0:
                S.op("pool", lambda e, i=i, i2=i2: e.tensor_copy(out=kdb2[i2][:, 1, :], in_=KDB[:, i, :]), [b_KDB[i]], [b_kdb2[i2]])

        def p2back(i):
            tok = slice(i * 128, (i + 1) * 128)
            i2 = i % 2
            S.op("dve", lambda e: e.tensor_copy(out=SBb, in_=SB_), [b_SB], [b_SBb])
            po, _, pob = K.psum()

            def fo(e, po=po, i=i, i2=i2, tok=tok):
                ins = None
                for h in range(4):
                    pr = h // 2
                    o_ = po[:, h * 128:(h + 1) * 128]
                    e.matmul(o_, lhsT=AM[i2][:, 0, h, :], rhs=VG[:, i, h * 128:(h + 1) * 128], start=True, stop=False)
                    e.matmul(o_, lhsT=AM[i2][:, 1, h, :], rhs=VG[:, i, h * 128:(h + 1) * 128], start=False, stop=False)
                    e.matmul(o_, lhsT=QEm[i2][:, 0, h, :], rhs=SFP[:, i, pr, :], start=False, stop=False)
                    ins = e.matmul(o_, lhsT=QEm[i2][:, 1, h, :], rhs=SBb[:, pr, :], start=False, stop=True)
                return ins
            S.op("pe", fo, [b_AM[i2], b_VG[i], b_QEm[i2], b_SFP[i], b_SBb], [pob])
            st_, sb_ = smslot()
            S.op("act", lambda e, po=po: e.activation(out=sq, in_=po[:, 0:512], func=AF.Square), [pob], [b_sq])
            S.op("dve", lambda e, st_=st_: e.tensor_reduce(out=st_[:, 0:4], in_=sq.rearrange("p (h d) -> p h d", h=4), axis=AX.X, op=ALU.add), [b_sq], [sb_])
            S.op("dve", lambda e, st_=st_: e.tensor_scalar(out=st_[:, 0:4], in0=st_[:, 0:4], scalar1=1.0 / 128, scalar2=EPS, op0=ALU.mult, op1=ALU.add), [sb_], [sb_])
            S.op("pool", lambda e, st_=st_: e.tensor_tensor(out=st_[:, 0:4], in0=st_[:, 0:4], in1=smallc[:, 2:6], op=ALU.pow), [sb_, B_const], [sb_])
            S.op("dve", lambda e, st_=st_, po=po: e.tensor_tensor(out=o32.rearrange("p (h d) -> p h d", h=4), in0=po.rearrange("p (h d) -> p h d", h=4),
                                                                  in1=st_[:, 0:4].unsqueeze(2).to_broadcast([128, 4, 128]), op=ALU.mult), [pob, sb_], [b_o32])
            S.op("pool", lambda e: e.tensor_tensor(out=o32.rearrange("p (h d) -> p h d", h=4), in0=o32.rearrange("p (h d) -> p h d", h=4),
                                                   in1=glaw[:].unsqueeze(1).to_broadcast([128, 4, 128]), op=ALU.mult), [b_o32, B_const], [b_o32])
            S.op("dve", lambda e, i2=i2: e.tensor_tensor(out=ogl[i2], in0=o32, in1=ogs[i2], op=ALU.mult), [b_o32, b_ogs[i2]], [b_ogl[i2]])
            qb_, qi = divmod(i, NQS)
            sslot = qb_ % 2
            transpose4(ogl[i2], b_ogl[i2], stg[sslot][:, :, qi * 128:(qi + 1) * 128], b_stg[sslot])
            if qi == 0:
                half = (qb_ * QB) // HB
                S.dma("pool", MIXT_d[u, :, 0:4, qb_ * QB:(qb_ + 1) * QB], stg[sslot], [b_stg[sslot]], [b_MIX[u][half * 2 + 0]])
            if i > 0:
                ((pub, pubb),) = u_matmuls(kdb2[i2], b_kdb2[i2], VG[:, i, :], b_VG[i], (1,))
                for h in range(4):
                    pr, lo = h // 2, (h % 2) * 64
                    S.op("dve", lambda e, h=h, pr=pr, lo=lo, pub=pub, i=i: e.scalar_tensor_tensor(
                        out=SB_[lo:lo + 64, pr, :], in0=SB_[lo:lo + 64, pr, :], scalar=DECB[lo:lo + 64, i, pr:pr + 1],
                        in1=pub[lo:lo + 64, h * 128:(h + 1) * 128], op0=ALU.mult, op1=ALU.add), [pubb, b_DECB[i], b_SB, b_SBb], [b_SB])
        p2front(NT - 1)
        for i in range(NT - 1, -1, -1):
            if i > 0:
                p2front(i - 1)
            p2back(i)

    def phase_D(u, is_prompt, qcs, qsc):
        K.arena_reset()
        WD_ = K.ar([128, KC, 1536], BF16)
        b_WDi = Buf()
        cols = (0,) if is_prompt else (0, 512, 1024)
        for c0 in cols:
            S.dma("sp", WD_[:, :, c0:c0 + 512], WIN_d[:, :, 1568 + c0:1568 + c0 + 512], [b_W["WIN"]], [b_WDi])
        QT = K.ar([128, 4, UT], BF16)
        b_QT = bufs(NT)
        if not is_prompt:
            KT = K.ar([128, 4, UT], BF16)
            VP = K.ar([128, 4, NT, VW], BF16)
            b_KT, b_VP = Buf(), Buf()
            S.op("pool", lambda e: e.memset(VP[:], 1.0), (), [b_VP])
        else:
            KTst = [K.ar([128, UT], BF16) for _ in range(2)]
            VPst = [K.ar([128, NT, VW], BF16) for _ in range(2)]
            b_st = bufs(2)
        rotq = [K.ar([128, 512], BF16) for _ in range(2)]
        rotk = [K.ar([128, 512], BF16) for _ in range(2)]
        b_rotq, b_rotk = bufs(2), bufs(2)
        rtmp = [K.ar([128, 256], F32) for _ in range(2)]
        b_rtmp = bufs(2)

        def dfront(i):
            tok = slice(i * 128, (i + 1) * 128)
            hTi = hT[:, :, tok]
            r = i % 2
            pq, _, pqb = K.psum()
            S.op("pe", proj(pq, hTi, WD_, 0, 512), [b_hT[i], b_WDi], [pqb])
            rope(pq, pqb, rotq[r], b_rotq[r], qcs[:, i, :], qsc[:, i, :], rtmp[0], b_rtmp[0], 0.125)
            if not is_prompt:
                pk, _, pkb = K.psum()
                S.op("pe", proj(pk, hTi, WD_, 512, 512), [b_hT[i], b_WDi], [pkb])
                rope(pk, pkb, rotk[r], b_rotk[r], tabs[:, 2, i, :], tabs[:, 3, i, :], rtmp[1], b_rtmp[1], None)
                pv, _, pvb = K.psum()
                S.op("pe", proj(pv, hTi, WD_, 1024, 512), [b_hT[i], b_WDi], [pvb])
                copy_op(K, "dve", VP[:, :, i, 0:128], pv.rearrange("p (h d) -> p h d", h=4), [pvb], [b_VP])

        def dback(i):
            tok = slice(i * 128, (i + 1) * 128)
            r = i % 2
            transpose4(rotq[r], b_rotq[r], QT[:, :, tok], b_QT[i])
            if not is_prompt:
                transpose4(rotk[r], b_rotk[r], KT[:, :, tok], b_KT)
        dfront(0)
        for i in range(NT):
            if i + 1 < NT:
                dfront(i + 1)
            dback(i)
        accb = [5, 6, 7]
        K.ps_reserved = set(accb)
        b_accs = [K.psb[b_] for b_ in accb]
        o32 = K.ar([128, 128], F32)
        t1 = K.ar([128, 128], F32)
        b_o32, b_t1 = Buf(), Buf()
        OD = [K.ar([128, NQS, 128], BF16) for _ in range(2)]
        b_OD = bufs(2)
        ODT = [K.ar([128, QB], BF16) for _ in range(2)]
        b_ODT = bufs(2)
        nseg = cfg.NC if is_prompt else 1
        it = 0
        ld = 0

        def acc_ap(c, qs, n=129):
            a = c * NQS + qs
            return K.ps[accb[a // 3]][:, (a % 3) * 129:(a % 3) * 129 + n]
        PTL = [K.ar([128, QB], BF16) for _ in range(8)]
        b_PTL = bufs(8)
        ptl_i = [0]
        accS = [K.ar([128, 3, 512], F32) for _ in range(2)]
        b_accS = bufs(2)
        sq32 = K.ar([128, 128], F32)
        b_sq32 = Buf()
        LOOK = 1
        for h in range(4):
            for qb_ in range(UT // QB):
                qsl = slice(qb_ * QB, (qb_ + 1) * QB)
                qbufs = [b_QT[qb_ * NQS + j] for j in range(NQS)]

                def zf(e):
                    ins = None
                    for b_ in accb:
                        ins = e.matmul(K.ps[b_][:, 0:512], lhsT=zerosb[:, 0:128], rhs=zerosb[:, 0:512], start=True, stop=True)
                    return ins
                pend = []
                zdone = [False]

                def emit_pv(item, zf=zf, zdone=zdone):
                    pts, vp_ap, kvb, kt, last = item
                    if not zdone[0]:
                        S.op("pe", zf, [B_const], b_accs)
                        zdone[0] = True

                    def fpv(e, pts=pts, vp_ap=vp_ap, kt=kt, last=last):
                        ins = None
                        for c in range(2):
                            for qs in range(NQS):
                                ins = e.matmul(acc_ap(c, qs), lhsT=pts[c][0][:, qs * 128:(qs + 1) * 128], rhs=vp_ap[:, kt, 0:129],
                                               start=False, stop=last, skip_group_check=True)
                        return ins
                    S.op("pe", fpv, [pts[0][1], pts[1][1]] + kvb, b_accs)
                for sg_ in range(nseg):
                    if is_prompt:
                        sl = ld % 2
                        ld += 1
                        S.dma("sp", KTst[sl], KT_d[sg_, h], [b_KV[sg_]], [b_st[sl]])
                        S.dma("sp", VPst[sl].rearrange("p a b -> p (a b)"), VP_d[sg_, h], [b_KV[sg_]], [b_st[sl]])
                        kt_ap, vp_ap, kvb = KTst[sl], VPst[sl], [b_st[sl]]
                    else:
                        kt_ap, vp_ap, kvb = KT[:, h, :], VP[:, h], [b_KT, b_VP]
                    for kt in range(NT):
                        last = (sg_ == nseg - 1 and kt == NT - 1)
                        pts = []
                        for c in range(2):
                            pf, _, pbuf = K.psum()
                            S.op("pe", lambda e, pf=pf, c=c, kt=kt, kt_ap=kt_ap, h=h, qsl=qsl: e.matmul(
                                pf[:, 0:QB], lhsT=kt_ap[c * 64:(c + 1) * 64, kt * 128:(kt + 1) * 128], rhs=QT[c * 64:(c + 1) * 64, h, qsl],
                                start=True, stop=True), kvb + qbufs, [pbuf])
                            pi_ = ptl_i[0]
                            ptl_i[0] = (pi_ + 1) % 8
                            pt_, ptb_ = PTL[pi_], b_PTL[pi_]
                            S.op("act", lambda e, pf=pf, pt_=pt_: e.activation(out=pt_, in_=pf[:, 0:QB], func=AF.Exp), [pbuf], [ptb_])
                            pts.append((pt_, ptb_))
                        pend.append((pts, vp_ap, kvb, kt, last))
                        if len(pend) > LOOK:
                            emit_pv(pend.pop(0))
                while pend:
                    emit_pv(pend.pop(0))
                o_i = it % 2
                it += 1
                aS = accS[o_i]
                for j_, b_ in enumerate(accb):
                    copy_op(K, "dve" if j_ != 1 else "act", aS[:, j_, :], K.ps[b_][:, 0:512], [K.psb[b_]], [b_accS[o_i]])

                def accs_ap(c, qs):
                    a_ = c * NQS + qs
                    return aS[:, a_ // 3, (a_ % 3) * 129:(a_ % 3) * 129 + 129]
                for qs in range(NQS):
                    st_, sb_ = smslot()
                    a0, a1 = accs_ap(0, qs), accs_ap(1, qs)
                    rb = [b_accS[o_i]]
                    S.op("dve", lambda e, st_=st_, a0=a0: e.reciprocal(out=st_[:, 0:1], in_=a0[:, 128:129]), rb, [sb_])
                    S.op("dve", lambda e, st_=st_, a1=a1: e.reciprocal(out=st_[:, 1:2], in_=a1[:, 128:129]), rb + [sb_], [sb_])
                    S.op("dve", lambda e, st_=st_, a1=a1: e.tensor_scalar(out=t1, in0=a1[:, 0:128], scalar1=st_[:, 1:2], scalar2=smallc[:, 0:1],
                                                                         op0=ALU.mult, op1=ALU.mult), rb + [sb_, B_const], [b_t1])
                    S.op("dve", lambda e, st_=st_, a0=a0: e.scalar_tensor_tensor(out=o32, in0=a0[:, 0:128], scalar=st_[:, 0:1], in1=t1,
                                                                                op0=ALU.mult, op1=ALU.add), rb + [sb_, b_t1], [b_o32])
                    S.op("dve", lambda e: e.tensor_tensor(out=sq32, in0=o32, in1=o32, op=ALU.mult), [b_o32], [b_sq32])
                    S.op("dve", lambda e, st_=st_: e.tensor_reduce(out=st_[:, 2:3], in_=sq32, axis=AX.X, op=ALU.add), [b_sq32, sb_], [sb_])
                    rstd_from(st_[:, 2:3], 128, st_[:, 3:4], [sb_], [sb_])
                    S.op("dve", lambda e, st_=st_, qs=qs, o_i=o_i: e.scalar_tensor_tensor(out=OD[o_i][:, qs, :], in0=o32, scalar=st_[:, 3:4], in1=subw[:],
                                                                                     op0=ALU.mult, op1=ALU.mult), [b_o32, sb_, B_const], [b_OD[o_i]])
                pf, pbf, pbuf = K.psum()

                def trd(e, pbf=pbf, o_i=o_i):
                    ins = None
                    for qs in range(NQS):
                        ins = e.transpose(out=pbf[:, qs * 128:(qs + 1) * 128], in_=OD[o_i][:, qs, :], identity=identb[:])
                    return ins
                S.op("pe", trd, [b_OD[o_i], B_const], [pbuf])
                copy_op(K, "dve", ODT[o_i], pbf[:, 0:QB], [pbuf], [b_ODT[o_i]])
                half = (qb_ * QB) // HB
                S.dma("pool", MIXT_d[u, :, 4 + h, qsl], ODT[o_i], [b_ODT[o_i]], [b_MIX[u][half * 2 + 1]])
        K.ps_reserved = set()

    def phase_M(msrc):
        K.arena_reset()
        Wkv = [K.ar([128, KC, 512], BF16) for _ in range(2)]
        b_Wkv = bufs(2)
        mT = K.ar([128, KC, MEM], BF16)
        b_mT = bufs(2)
        for mt in range(2):
            load_norm_T(msrc[mt * 128:(mt + 1) * 128, :], mT[:, :, mt * 128:(mt + 1) * 128], b_mT[mt])
        for cb in range(4):
            w_ = Wkv[cb % 2]
            S.dma("sp", w_, WXKV_d[:, :, cb * 512:(cb + 1) * 512], [b_W["WXKV"]], [b_Wkv[cb % 2]])
            if cb < 2:
                for j in range(4):
                    pf, _, pbuf = K.psum()

                    def fk(e, pf=pf, j=j, w_=w_):
                        ins = None
                        for c in range(KC):
                            ins = e.matmul(pf[:, 0:MEM], lhsT=w_[:, c, j * 128:(j + 1) * 128], rhs=mT[:, c, :], start=(c == 0), stop=(c == KC - 1))
                        return ins
                    S.op("pe", fk, [b_Wkv[cb % 2]] + b_mT, [pbuf])
                    copy_op(K, K.alt(), KMT[:, cb * 4 + j, :], pf[:, 0:MEM], [pbuf], [b_KM])
            else:
                for mt in range(2):
                    pf, _, pbuf = K.psum()

                    def fv(e, pf=pf, mt=mt, w_=w_):
                        ins = None
                        for c in range(KC):
                            ins = e.matmul(pf[:, 0:512], lhsT=mT[:, c, mt * 128:(mt + 1) * 128], rhs=w_[:, c, :], start=(c == 0), stop=(c == KC - 1))
                        return ins
                    S.op("pe", fv, [b_Wkv[cb % 2]] + b_mT, [pbuf])
                    hh = (cb - 2) * 2
                    copy_op(K, K.alt(), VMP[:, mt, hh:hh + 2, 0:256], pf[:, 0:512].rearrange("p (h d) -> p h d", h=2), [pbuf], [b_KM])

    pn_tmp = K.sb("pn_tmp", [128, 2, 512], F32)
    b_pn = bufs(2)
    pn_i = [0]

    def post_norm_res(pbanks, pbufs, xres, xres_b, out_ap, out_b, wk):
        st_, sb_ = smslot()
        for j in range(2):
            S.op("act", lambda e, j=j, st_=st_: e.activation(out=junk[:, 0:512], in_=pbanks[j][:, 0:512], func=AF.Square, accum_out=st_[:, j:j + 1]),
                 [pbufs[j], sb_], [b_junk, sb_])
        S.op("dve", lambda e, st_=st_: e.tensor_tensor(out=st_[:, 2:3], in0=st_[:, 0:1], in1=st_[:, 1:2], op=ALU.add), [sb_], [sb_])
        rstd_from(st_[:, 2:3], D, st_[:, 3:4], [sb_], [sb_])
        for j in range(2):
            sl = slice(j * 512, (j + 1) * 512)
            k = pn_i[0] % 2
            pn_i[0] += 1
            S.op("dve", lambda e, j=j, sl=sl, st_=st_, k=k: e.scalar_tensor_tensor(out=pn_tmp[:, k, :], in0=pbanks[j][:, 0:512], scalar=st_[:, 3:4], in1=wpost[:, wk, sl],
                                                                                 op0=ALU.mult, op1=ALU.mult), [pbufs[j], sb_, B_const], [b_pn[k]])
            S.op("pool", lambda e, sl=sl, k=k: e.tensor_tensor(out=out_ap[:, sl], in0=pn_tmp[:, k, :], in1=xres[:, sl], op=ALU.add), [b_pn[k], xres_b], [out_b])

    def phase_T(u, half, xsrc, ydst):
        K.arena_reset()
        HT_ = HB // 128
        t0 = half * HT_
        X1 = K.ar([128, HT_, D], F32)
        b_X1 = bufs(HT_)
        hTh = hT[:, :, 0:HB]
        b_hTh = bufs(HT_)
        mark = K.arena_off
        MIXh = K.ar([128, KC, HB], BF16)
        b_MIXh = Buf()
        S.dma("sp", MIXh[:, 0:4, :], MIXT_d[u, :, 0:4, half * HB:(half + 1) * HB], [b_MIX[u][half * 2 + 0]], [b_MIXh])
        S.dma("sp", MIXh[:, 4:8, :], MIXT_d[u, :, 4:8, half * HB:(half + 1) * HB], [b_MIX[u][half * 2 + 1]], [b_MIXh])
        Wa = K.ar([128, KC, D], BF16)
        Wb = K.ar([128, KC, D], BF16)
        b_Wa, b_Wb = Buf(), Buf()
        S.dma("sp", Wa, WOUT_d[:, :, :], [b_W["WOUT"]], [b_Wa])
        S.dma("sp", Wb, WXQ_d[:, :, :], [b_W["WXQ"]], [b_Wb])
        XQT = K.ar([128, KC, QB], BF16)
        b_XQT = Buf()
        OX = K.ar([128, NQS, D], BF16)
        b_OX = bufs(NQS)
        PTE = [K.ar([128, QB], BF16) for _ in range(8)]
        b_PTE = bufs(8)
        pte_i = [0]
        for tl in range(HT_):
            tg = t0 + tl
            tok = slice(tl * 128, (tl + 1) * 128)
            xa, xb_ = xslot()
            S.dma("sp", xa, xsrc[tg * 128:(tg + 1) * 128, :], (), [xb_])
            pp = [K.psum() for _ in range(2)]
            for nb in range(2):
                def fo(e, nb=nb, tok=tok, pf=pp[nb][0]):
                    ins = None
                    for c in range(KC):
                        ins = e.matmul(pf[:, 0:512], lhsT=MIXh[:, c, tok], rhs=Wa[:, c, nb * 512:(nb + 1) * 512], start=(c == 0), stop=(c == KC - 1))
                    return ins
                S.op("pe", fo, [b_MIXh, b_Wa], [pp[nb][2]])
            post_norm_res([pp[0][0], pp[1][0]], [pp[0][2], pp[1][2]], xa, xb_, X1[:, tl, :], b_X1[tl], 0)
        S.dma("sp", Wa, WXO_d[:, :, :], [b_W["WXO"]], [b_Wa])
        for tl in range(HT_):
            norm_T(X1[:, tl, :], b_X1[tl], hTh[:, :, tl * 128:(tl + 1) * 128], b_hTh[tl])
        for qb_ in range(HB // QB):
            tls = [qb_ * NQS + j for j in range(NQS)]
            qsl = slice(qb_ * QB, (qb_ + 1) * QB)
            for cc in range(KC):
                pf, _, pbuf = K.psum()

                def fq(e, pf=pf, cc=cc, qsl=qsl):
                    ins = None
                    for c in range(KC):
                        ins = e.matmul(pf[:, 0:QB], lhsT=Wb[:, c, cc * 128:(cc + 1) * 128], rhs=hTh[:, c, qsl], start=(c == 0), stop=(c == KC - 1))
                    return ins
                S.op("pe", fq, [b_Wb] + [b_hTh[t] for t in tls], [pbuf])
                copy_op(K, K.alt(), XQT[:, cc, :], pf[:, 0:QB], [pbuf], [b_XQT])
            allpts = []
            for h in range(4):
                pts = []
                for mt in range(2):
                    pf, _, pbuf = K.psum()

                    def fs(e, pf=pf, mt=mt, h=h):
                        e.matmul(pf[:, 0:QB], lhsT=KMT[:, 2 * h, mt * 128:(mt + 1) * 128], rhs=XQT[:, 2 * h, :], start=True, stop=False)
                        return e.matmul(pf[:, 0:QB], lhsT=KMT[:, 2 * h + 1, mt * 128:(mt + 1) * 128], rhs=XQT[:, 2 * h + 1, :], start=False, stop=True)
                    S.op("pe", fs, [b_KM, b_XQT], [pbuf])
                    pe_i = pte_i[0]
                    pte_i[0] = (pe_i + 1) % 8
                    pt_, ptb_ = PTE[pe_i], b_PTE[pe_i]
                    S.op("act", lambda e, pf=pf, pt_=pt_: e.activation(out=pt_, in_=pf[:, 0:QB], func=AF.Exp, scale=1.0 / 16), [pbuf], [ptb_])
                    pts.append((pt_, ptb_))
                allpts.append(pts)
            for h in range(4):
                pts = allpts[h]
                for qs in range(NQS):
                    pf, _, pbuf = K.psum()

                    def fpv(e, pf=pf, qs=qs, h=h, pts=pts):
                        e.matmul(pf[:, 0:257], lhsT=pts[0][0][:, qs * 128:(qs + 1) * 128], rhs=VMP[:, 0, h, 0:257], start=True, stop=False)
                        return e.matmul(pf[:, 0:257], lhsT=pts[1][0][:, qs * 128:(qs + 1) * 128], rhs=VMP[:, 1, h, 0:257], start=False, stop=True)
                    S.op("pe", fpv, [pts[0][1], pts[1][1], b_KM], [pbuf])
                    st_, sb_ = smslot()
                    S.op("dve", lambda e, st_=st_, pf=pf: e.reciprocal(out=st_[:, 0:1], in_=pf[:, 256:257]), [pbuf], [sb_])
                    S.op("dve", lambda e, st_=st_, pf=pf, qs=qs, h=h: e.tensor_scalar(out=OX[:, qs, h * 256:(h + 1) * 256], in0=pf[:, 0:256], scalar1=st_[:, 0:1],
                                                                                   scalar2=None, op0=ALU.mult), [pbuf, sb_], [b_OX[qs]])
            for j, tl in enumerate(tls):
                tok = slice(tl * 128, (tl + 1) * 128)
                pf, pbf, pbuf = K.psum()

                def trx(e, pbf=pbf, j=j):
                    ins = None
                    for c in range(KC):
                        ins = e.transpose(out=pbf[:, c * 128:(c + 1) * 128], in_=OX[:, j, c * 128:(c + 1) * 128], identity=identb[:])
                    return ins
                S.op("pe", trx, [b_OX[j], B_const], [pbuf])
                copy_op(K, K.alt(), hTh[:, :, tok], pbf.rearrange("p (a b) -> p a b", a=KC), [pbuf, b_XQT], [b_hTh[tl]])
                pp = [K.psum() for _ in range(2)]
                for nb in range(2):
                    def fx(e, nb=nb, tok=tok, pf=pp[nb][0]):
                        ins = None
                        for c in range(KC):
                            ins = e.matmul(pf[:, 0:512], lhsT=hTh[:, c, tok], rhs=Wa[:, c, nb * 512:(nb + 1) * 512], start=(c == 0), stop=(c == KC - 1))
                        return ins
                    S.op("pe", fx, [b_hTh[tl], b_Wa], [pp[nb][2]])
                post_norm_res([pp[0][0], pp[1][0]], [pp[0][2], pp[1][2]], X1[:, tl, :], b_X1[tl], X1[:, tl, :], b_X1[tl], 1)
        S.barrier()
        K.arena_off = mark
        HID = K.ar([128, FC, HB], BF16)
        b_HID = bufs(FC)
        GU = [K.ar([128, 2, KC, 128], BF16) for _ in range(3)]
        b_GU = bufs(3)
        WDr = [K.ar([128, 512], BF16) for _ in range(6)]
        b_WDr = bufs(6)
        YR = K.ar([128, 4, D], F32)
        b_YR = bufs(4)
        sgl = [K.ar([128, 512], F32) for _ in range(1)]
        b_sgl = bufs(1)
        for tl in range(HT_):
            norm_T(X1[:, tl, :], b_X1[tl], hTh[:, :, tl * 128:(tl + 1) * 128], b_hTh[tl])
        k_ = 0
        for fc in range(FC):
            g_ = fc % 3
            S.dma("sp", GU[g_].rearrange("p a b c -> p (a b c)"), WGU_d[fc].rearrange("p a b c -> p (a b c)"), [b_W["WGU"]], [b_GU[g_]])
            for sb_i in range(HB // 512):
                tsl = slice(sb_i * 512, (sb_i + 1) * 512)
                tl4 = [sb_i * 4 + j for j in range(4)]
                pg = K.psum()
                pu = K.psum()
                for which, pz in ((0, pg), (1, pu)):
                    def fg(e, which=which, pf=pz[0], g_=g_, tsl=tsl):
                        ins = None
                        for c in range(KC):
                            ins = e.matmul(pf[:, 0:512], lhsT=GU[g_][:, which, c, :], rhs=hTh[:, c, tsl], start=(c == 0), stop=(c == KC - 1))
                        return ins
                    S.op("pe", fg, [b_GU[g_]] + [b_hTh[t] for t in tl4], [pz[2]])
                s_i = 0
                k_ += 1
                S.op("act", lambda e, s_i=s_i, pf=pg[0]: e.activation(out=sgl[s_i], in_=pf[:, 0:512], func=AF.Silu), [pg[2]], [b_sgl[s_i]])
                S.op("dve", lambda e, s_i=s_i, pf=pu[0], fc=fc, tsl=tsl: e.tensor_tensor(out=HID[:, fc, tsl], in0=pf[:, 0:512], in1=sgl[s_i], op=ALU.mult),
                     [pu[2], b_sgl[s_i]], [b_HID[fc]])
        wd_i = 0
        for grp in range(HT_ // 4):
            tls = [grp * 4 + j for j in range(4)]
            for nb in range(2):
                acc = [K.psum() for _ in range(4)]
                for fc in range(FC):
                    w_i = wd_i % 6
                    wd_i += 1
                    S.dma("sp", WDr[w_i], WD_d[fc, :, nb * 512:(nb + 1) * 512], [b_W["WD"]], [b_WDr[w_i]])
                    for j, tl in enumerate(tls):
                        S.op("pe", lambda e, j=j, tl=tl, fc=fc, w_i=w_i, pf=acc[j][0]: e.matmul(
                            pf[:, 0:512], lhsT=HID[:, fc, tl * 128:(tl + 1) * 128], rhs=WDr[w_i], start=(fc == 0), stop=(fc == FC - 1)),
                            [b_HID[fc], b_WDr[w_i]], [acc[j][2]])
                for j, tl in enumerate(tls):
                    st_ = sm[:, 48 + j * 4:52 + j * 4]
                    S.op("act", lambda e, j=j, nb=nb, st_=st_, pf=acc[j][0]: e.activation(out=YR[:, j, nb * 512:(nb + 1) * 512], in_=pf[:, 0:512], func=AF.Copy),
                         [acc[j][2]], [b_YR[j]])
                    S.op("act", lambda e, j=j, nb=nb, st_=st_: e.activation(out=junk[:, 0:512], in_=YR[:, j, nb * 512:(nb + 1) * 512], func=AF.Square,
                                                                           accum_out=st_[:, nb:nb + 1]), [b_YR[j], b_ysm[j]], [b_junk, b_ysm[j]])
            for j, tl in enumerate(tls):
                tg = t0 + tl
                st_ = sm[:, 48 + j * 4:52 + j * 4]
                S.op("dve", lambda e, st_=st_: e.tensor_tensor(out=st_[:, 2:3], in0=st_[:, 0:1], in1=st_[:, 1:2], op=ALU.add), [b_ysm[j]], [b_ysm[j]])
                rstd_from(st_[:, 2:3], D, st_[:, 3:4], [b_ysm[j]], [b_ysm[j]])
                for nb in range(2):
                    sl = slice(nb * 512, (nb + 1) * 512)
                    S.op("dve", lambda e, j=j, sl=sl, st_=st_: e.scalar_tensor_tensor(out=YR[:, j, sl], in0=YR[:, j, sl], scalar=st_[:, 3:4], in1=wpost[:, 2, sl],
                                                                                    op0=ALU.mult, op1=ALU.mult), [b_YR[j], b_ysm[j], B_const], [b_YR[j]])
                    S.op("pool", lambda e, j=j, sl=sl, tl=tl: e.tensor_tensor(out=YR[:, j, sl], in0=YR[:, j, sl], in1=X1[:, tl, sl], op=ALU.add),
                         [b_YR[j], b_X1[tl]], [b_YR[j]])
                S.dma("pool", ydst[tg * 128:(tg + 1) * 128, :], YR[:, j, :], [b_YR[j]], [b_yout])

    b_ysm = bufs(4)
    b_yout = Buf()

    import os as _os
    STOP = _os.environ.get("KSTOP", "")

    def finish():
        S.final_wait()
        S.emit(block, sems)
        st.close()
        return nc, K
    setup_consts()
    if STOP == "consts":
        return finish()
    K.arena_reset()
    wk1 = K.ar([128, 4, NT, 8], F32)
    make_tables(NT, False, tabs[:, 0], tabs[:, 1], 0.125, wk1)
    make_tables(NT, False, tabs[:, 2], tabs[:, 3], 1.0, wk1)
    make_tables(NT, True, tabs[:, 4], tabs[:, 5], 0.125, wk1)
    S.barrier()
    if STOP == "tables":
        return finish()
    weight_prep()
    S.barrier()
    if STOP == "wprep":
        return finish()
    pre_phase()
    S.barrier()
    if STOP == "pre":
        return finish()
    for u in range(NU):
        if u == 0:
            xsrc, msrc, ydst = xps, memp, yp
        else:
            xsrc = xs[(u - 1) * UT:u * UT, :]
            msrc = mems[(u - 1) * MEM:u * MEM, :]
            ydst = ys[(u - 1) * UT:u * UT, :]
        phase_G(u, xsrc)
        S.barrier()
        if STOP in ("G%d" % u, "G%da" % u):
            return finish()
        if u == 0:
            phase_D(u, True, tabs[:, 4], tabs[:, 5])
        else:
            phase_D(u, False, tabs[:, 0], tabs[:, 1])
        S.barrier()
        if STOP == "D%d" % u:
            return finish()
        phase_M(msrc)
        S.barrier()
        if STOP == "M%d" % u:
            return finish()
        for half in range(UT // HB):
            phase_T(u, half, xsrc, ydst)
            S.barrier()
    S.final_wait()
    S.emit(block, sems)
    st.close()
    return nc, K


_CACHE = {}


def _inputs_for_core(cfg, c, I):
    UT, NSQ, T, NT = cfg.UT, cfg.NSQ, cfg.T, cfg.NT
    m = {}
    xp = np.ascontiguousarray(I["x_prompt"][0])
    m["xpf"] = xp
    m["xps"] = np.ascontiguousarray(xp[c * UT:(c + 1) * UT])
    m["xs"] = np.ascontiguousarray(I["x_sample"][c * NSQ:(c + 1) * NSQ].reshape(NSQ * UT, D))
    m["memp"] = np.ascontiguousarray(I["mem_prompt"][0])
    m["mems"] = np.ascontiguousarray(I["mem_sample"][c * NSQ:(c + 1) * NSQ].reshape(NSQ * MEM, D))
    m["posb"] = np.full((128, 1), float(c * UT), np.float32)
    mk = np.zeros((128, 2, T), np.float32)
    mk[:, 0, :c * NT] = 1.0
    mk[:, 1, (c + 1) * NT:] = 1.0
    m["mk"] = mk.reshape(128, 2 * T)
    for n, k in [("n_mix_pre", "norm_mix_pre"), ("n_mix_post", "norm_mix_post"), ("n_x_pre", "norm_xattn_pre"), ("n_mem", "norm_mem"),
                 ("n_x_post", "norm_xattn_post"), ("n_f_pre", "norm_ffn_pre"), ("n_f_post", "norm_ffn_post"), ("bg_f", "b_gate_f"),
                 ("bg_b", "b_gate_b"), ("gla_nw", "gla_norm_w"), ("lq1", "lambda_q1"), ("lk1", "lambda_k1"), ("lq2", "lambda_q2"),
                 ("lk2", "lambda_k2"), ("sub_w", "diff_subln_w")]:
        m[n] = np.ascontiguousarray(I[k][0:1]).astype(np.float32)
    for n, k in [("w_in", "w_in"), ("wgu_f", "w_gate_up_f"), ("wgu_b", "w_gate_up_b"), ("w_out", "w_out"), ("w_xq", "w_xq"), ("w_xkv", "w_xkv"),
                 ("w_xo", "w_xo"), ("w_fg", "w_ffn_gate"), ("w_fu", "w_ffn_up"), ("w_fd", "w_ffn_down")]:
        m[n] = np.ascontiguousarray(I[k][0]).astype(np.float32)
    return m


def run(cfg, I, debug=False):
    key = (cfg.UT, cfg.NSQ, debug)
    if key not in _CACHE:
        _CACHE[key] = build(cfg, debug)
    nc, K = _CACHE[key]
    in_maps = [_inputs_for_core(cfg, c, I) for c in range(cfg.NC)]
    res = run_bass_kernel_spmd(nc, in_maps, core_ids=list(range(cfg.NC)))
    return res


def kernel(**inputs):
    cfg = Cfg(UT=2048, NSQ=4, NC=8)
    I = {k: np.asarray(v) for k, v in inputs.items()}
    res = run(cfg, I)
    yp = np.concatenate([r["yp"] for r in res.results], axis=0)[None]
    ys = np.concatenate([r["ys"].reshape(cfg.NSQ, cfg.UT, D) for r in res.results], axis=0)
    return (yp.astype(np.float32), ys.astype(np.float32))
```
